# Optimizing a Trainium2 kernel written in Bass

```python
import math
import jax, jax.numpy as jnp
from jax import lax
import numpy as np

D_MODEL = 2048
BATCH = 1
SEQ = 16384
DEPTH = 2

CONV_CH = 1024
CONV_K = 31
SSD_HEADS = 32
SSD_HEADDIM = 64
SSD_INNER = SSD_HEADS * SSD_HEADDIM
SSD_GROUPS = 4
SSD_STATE = 128
SSD_CONV_K = 4
SSD_CHUNK = 128
SSD_XBC = SSD_INNER + 2 * SSD_GROUPS * SSD_STATE
DA_HEADS = 8
DA_HEAD_DIM = 64
DA_V_DIM = 2 * DA_HEAD_DIM
DA_QK_WIDTH = DA_HEADS * 2 * DA_HEAD_DIM
DA_V_WIDTH = DA_HEADS * DA_V_DIM
ROPE_DIM = DA_HEAD_DIM // 4
ROPE_THETA = 500000.0
Q_BLOCK = 128
D_FF = 5632
FFN_CONV_K = 3
N_BRANCH = 3
RMS_EPS = 1e-6
LN_EPS = 1e-5
NEG_INF = -1e30

W_CONV_IN = 2 * CONV_CH
W_Z = SSD_INNER
W_DT = SSD_HEADS
W_GATES = N_BRANCH * D_MODEL
_O1 = W_CONV_IN
_O2 = _O1 + W_Z
_O3 = _O2 + SSD_XBC
_O4 = _O3 + W_DT
_O5 = _O4 + DA_QK_WIDTH
_O6 = _O5 + DA_QK_WIDTH
_O7 = _O6 + DA_V_WIDTH
IN_WIDTH = _O7 + W_GATES
IN_SPLITS = [_O1, _O2, _O3, _O4, _O5, _O6, _O7]

kernel_name = 'hybrid_gated_conformer_ssd_diffattn'


def rms_norm(x, w, eps=RMS_EPS):
    xf = x.astype(jnp.float32)
    y = xf * lax.rsqrt(jnp.mean(xf * xf, axis=-1, keepdims=True) + eps)
    return (y * w.astype(jnp.float32)).astype(x.dtype)


def layer_norm(x, w, b, eps=LN_EPS):
    xf = x.astype(jnp.float32)
    mu = jnp.mean(xf, axis=-1, keepdims=True)
    xc = xf - mu
    var = jnp.mean(xc * xc, axis=-1, keepdims=True)
    return (xc * lax.rsqrt(var + eps) * w.astype(jnp.float32) + b.astype(jnp.float32)).astype(x.dtype)


def causal_dwconv(x, w):
    k = w.shape[0]
    return lax.conv_general_dilated(
        x, w.astype(x.dtype)[:, None, :], window_strides=(1,), padding=[(k - 1, 0)],
        dimension_numbers=('NWC', 'WIO', 'NWC'), feature_group_count=x.shape[-1])


def rope_tables(positions):
    inv = 1.0 / (ROPE_THETA ** (jnp.arange(0, ROPE_DIM, 2, dtype=jnp.float32) / ROPE_DIM))
    ang = positions.astype(jnp.float32)[..., None] * inv
    return jnp.cos(ang), jnp.sin(ang)


def apply_partial_rope(t, cos, sin):
    half = ROPE_DIM // 2
    c = cos[:, :, None, None, :]
    s = sin[:, :, None, None, :]
    t1 = t[..., :half].astype(jnp.float32)
    t2 = t[..., half:ROPE_DIM].astype(jnp.float32)
    rot = jnp.concatenate([t1 * c - t2 * s, t2 * c + t1 * s], axis=-1).astype(t.dtype)
    return jnp.concatenate([rot, t[..., ROPE_DIM:]], axis=-1)


def conformer_conv_branch(u, w_dw, b_dw, ln_w, ln_b, w_out):
    a, g = jnp.split(u, 2, axis=-1)
    h = a * jax.nn.sigmoid(g)
    h = causal_dwconv(h, w_dw) + b_dw.astype(h.dtype)
    h = jax.nn.silu(layer_norm(h, ln_w, ln_b))
    return h @ w_out


def ssd_chunked_scan(x, dt, a, bm, cm):
    bsz, s_len, h, p = x.shape
    g, n = bm.shape[2], bm.shape[3]
    hg = h // g
    L = SSD_CHUNK
    nc = s_len // L
    xs = x.astype(jnp.float32).reshape(bsz, nc, L, g, hg, p).transpose(1, 0, 2, 3, 4, 5)
    dts = dt.reshape(bsz, nc, L, g, hg).transpose(1, 0, 2, 3, 4)
    bs = bm.astype(jnp.float32).reshape(bsz, nc, L, g, n).transpose(1, 0, 2, 3, 4)
    cs_ = cm.astype(jnp.float32).reshape(bsz, nc, L, g, n).transpose(1, 0, 2, 3, 4)
    ag = a.reshape(g, hg)
    tri = jnp.tril(jnp.ones((L, L), dtype=bool))[None, :, :, None, None]

    def step(state, inp):
        xc, dtc, bc, cc = inp
        cum = jnp.cumsum(dtc * ag, axis=1)
        seg = cum[:, :, None] - cum[:, None, :]
        decay = jnp.exp(jnp.where(tri, seg, -jnp.inf))
        cb = jnp.einsum('btgn,bsgn->btsg', cc, bc)
        wts = cb[..., None] * decay * dtc[:, None]
        y_diag = jnp.einsum('btsgh,bsghp->btghp', wts, xc)
        y_off = jnp.einsum('btgn,bghpn->btghp', cc, state) * jnp.exp(cum)[..., None]
        decay_end = jnp.exp(cum[:, -1:] - cum) * dtc
        new_state = state * jnp.exp(cum[:, -1])[..., None, None] + jnp.einsum(
            'bsgh,bsgn,bsghp->bghpn', decay_end, bc, xc)
        return new_state, y_diag + y_off

    state0 = jnp.zeros((bsz, g, hg, p, n), jnp.float32)
    _, ys = lax.scan(step, state0, (xs, dts, bs, cs_))
    return ys.transpose(1, 0, 2, 3, 4, 5).reshape(bsz, s_len, h, p)


def mamba2_branch(z, xbc, dt, conv_w, conv_b, dt_bias, a_log, d_skip, norm_w, w_out):
    bsz, s_len, _ = z.shape
    xbc = jax.nn.silu(causal_dwconv(xbc, conv_w) + conv_b.astype(xbc.dtype))
    xs, bm, cm = jnp.split(xbc, [SSD_INNER, SSD_INNER + SSD_GROUPS * SSD_STATE], axis=-1)
    xs = xs.reshape(bsz, s_len, SSD_HEADS, SSD_HEADDIM)
    bm = bm.reshape(bsz, s_len, SSD_GROUPS, SSD_STATE)
    cm = cm.reshape(bsz, s_len, SSD_GROUPS, SSD_STATE)
    dtf = jax.nn.softplus(dt.astype(jnp.float32) + dt_bias.astype(jnp.float32))
    a = -jnp.exp(a_log.astype(jnp.float32))
    y = ssd_chunked_scan(xs, dtf, a, bm, cm) + xs.astype(jnp.float32) * d_skip.astype(jnp.float32)[:, None]
    y = y.astype(z.dtype).reshape(bsz, s_len, SSD_INNER) * jax.nn.silu(z)
    gs = SSD_INNER // SSD_GROUPS
    y = rms_norm(y.reshape(bsz, s_len, SSD_GROUPS, gs), norm_w.reshape(SSD_GROUPS, gs))
    return y.reshape(bsz, s_len, SSD_INNER) @ w_out


def diff_attn_branch(q, k, v, cos, sin, lq1, lk1, lq2, lk2, subln_w, w_out, lambda_init):
    bsz, s_len, _ = q.shape
    q = apply_partial_rope(q.reshape(bsz, s_len, DA_HEADS, 2, DA_HEAD_DIM), cos, sin)
    k = apply_partial_rope(k.reshape(bsz, s_len, DA_HEADS, 2, DA_HEAD_DIM), cos, sin)
    v = v.reshape(bsz, s_len, DA_HEADS, DA_V_DIM)
    lam = (jnp.exp(jnp.sum(lq1.astype(jnp.float32) * lk1.astype(jnp.float32)))
           - jnp.exp(jnp.sum(lq2.astype(jnp.float32) * lk2.astype(jnp.float32))) + lambda_init)
    scale = DA_HEAD_DIM ** -0.5
    nblk = s_len // Q_BLOCK
    qb = q.reshape(bsz, nblk, Q_BLOCK, DA_HEADS, 2, DA_HEAD_DIM).transpose(1, 0, 2, 3, 4, 5)
    kpos = jnp.arange(s_len)

    def block(args):
        q_blk, i = args
        sc = jnp.einsum('bqhcd,bkhcd->bhcqk', q_blk, k).astype(jnp.float32) * scale
        qpos = i * Q_BLOCK + jnp.arange(Q_BLOCK)
        mask = kpos[None, :] <= qpos[:, None]
        prob = jax.nn.softmax(jnp.where(mask, sc, NEG_INF), axis=-1)
        diff = prob[:, :, 0] - lam * prob[:, :, 1]
        return jnp.einsum('bhqk,bkhd->bqhd', diff.astype(v.dtype), v)

    o = lax.map(block, (qb, jnp.arange(nblk)))
    o = o.transpose(1, 0, 2, 3, 4).reshape(bsz, s_len, DA_HEADS, DA_V_DIM)
    o = rms_norm(o, subln_w, eps=LN_EPS) * (1.0 - lambda_init)
    return o.reshape(bsz, s_len, DA_V_WIDTH) @ w_out


def conv_glu_ffn(h, w_up, w_dw, w_down):
    gate, up = jnp.split(h @ w_up, 2, axis=-1)
    gate = causal_dwconv(gate, w_dw)
    return (jax.nn.silu(gate) * up) @ w_down


def _normal(k, shape, scale):
    return jax.random.normal(k, shape, jnp.float32) * scale


def setup_inputs(seed: int = 0) -> dict:
    key = jax.random.key(seed)
    ks = jax.random.split(key, 32)
    L = DEPTH
    dt0 = jnp.exp(jax.random.uniform(ks[11], (L, SSD_HEADS), jnp.float32, math.log(1e-3), math.log(1e-1)))
    return {
        'x': _normal(ks[0], (BATCH, SEQ, D_MODEL), 1.0),
        'positions': jnp.broadcast_to(jnp.arange(SEQ, dtype=jnp.int32), (BATCH, SEQ)),
        'norm1_w': 1.0 + _normal(ks[1], (L, D_MODEL), 0.02),
        'w_in': _normal(ks[2], (L, D_MODEL, IN_WIDTH), D_MODEL ** -0.5),
        'gate_b': _normal(ks[3], (L, N_BRANCH, D_MODEL), 0.01),
        'conv_dw_w': _normal(ks[4], (L, CONV_K, CONV_CH), CONV_K ** -0.5),
        'conv_dw_b': _normal(ks[5], (L, CONV_CH), 0.01),
        'conv_ln_w': 1.0 + _normal(ks[6], (L, CONV_CH), 0.02),
        'conv_ln_b': _normal(ks[7], (L, CONV_CH), 0.01),
        'conv_out_w': _normal(ks[8], (L, CONV_CH, D_MODEL), CONV_CH ** -0.5),
        'ssd_conv_w': _normal(ks[9], (L, SSD_CONV_K, SSD_XBC), SSD_CONV_K ** -0.5),
        'ssd_conv_b': _normal(ks[10], (L, SSD_XBC), 0.01),
        'ssd_dt_bias': dt0 + jnp.log(-jnp.expm1(-dt0)),
        'ssd_a_log': jnp.log(jax.random.uniform(ks[12], (L, SSD_HEADS), jnp.float32, 1.0, 16.0)),
        'ssd_d': 1.0 + _normal(ks[13], (L, SSD_HEADS), 0.02),
        'ssd_norm_w': 1.0 + _normal(ks[14], (L, SSD_INNER), 0.02),
        'ssd_out_w': _normal(ks[15], (L, SSD_INNER, D_MODEL), SSD_INNER ** -0.5),
        'da_lambda_q1': _normal(ks[16], (L, DA_HEAD_DIM), 0.1),
        'da_lambda_k1': _normal(ks[17], (L, DA_HEAD_DIM), 0.1),
        'da_lambda_q2': _normal(ks[18], (L, DA_HEAD_DIM), 0.1),
        'da_lambda_k2': _normal(ks[19], (L, DA_HEAD_DIM), 0.1),
        'da_subln_w': 1.0 + _normal(ks[20], (L, DA_V_DIM), 0.02),
        'da_out_w': _normal(ks[21], (L, DA_V_WIDTH, D_MODEL), DA_V_WIDTH ** -0.5),
        'w_o': _normal(ks[22], (L, D_MODEL, D_MODEL), D_MODEL ** -0.5),
        'norm2_w': 1.0 + _normal(ks[23], (L, D_MODEL), 0.02),
        'ffn_up_w': _normal(ks[24], (L, D_MODEL, 2 * D_FF), D_MODEL ** -0.5),
        'ffn_dw_w': _normal(ks[25], (L, FFN_CONV_K, D_FF), FFN_CONV_K ** -0.5),
        'ffn_down_w': _normal(ks[26], (L, D_FF, D_MODEL), D_FF ** -0.5),
        'final_norm_w': 1.0 + _normal(ks[27], (D_MODEL,), 0.02),
    }


def reference(x, positions, norm1_w, w_in, gate_b, conv_dw_w, conv_dw_b, conv_ln_w, conv_ln_b, conv_out_w,
              ssd_conv_w, ssd_conv_b, ssd_dt_bias, ssd_a_log, ssd_d, ssd_norm_w, ssd_out_w,
              da_lambda_q1, da_lambda_k1, da_lambda_q2, da_lambda_k2, da_subln_w, da_out_w,
              w_o, norm2_w, ffn_up_w, ffn_dw_w, ffn_down_w, final_norm_w):
    bsz, s_len, _ = x.shape
    cos, sin = rope_tables(positions)
    for l in range(DEPTH):
        lambda_init = 0.8 - 0.6 * math.exp(-0.3 * l)
        xn = rms_norm(x, norm1_w[l])
        u = xn @ w_in[l]
        u_conv, u_z, u_xbc, u_dt, u_q, u_k, u_v, u_gate = jnp.split(u, IN_SPLITS, axis=-1)
        y_a = conformer_conv_branch(u_conv, conv_dw_w[l], conv_dw_b[l], conv_ln_w[l], conv_ln_b[l], conv_out_w[l])
        y_b = mamba2_branch(u_z, u_xbc, u_dt, ssd_conv_w[l], ssd_conv_b[l], ssd_dt_bias[l], ssd_a_log[l],
                            ssd_d[l], ssd_norm_w[l], ssd_out_w[l])
        y_c = diff_attn_branch(u_q, u_k, u_v, cos, sin, da_lambda_q1[l], da_lambda_k1[l], da_lambda_q2[l],
                               da_lambda_k2[l], da_subln_w[l], da_out_w[l], lambda_init)
        gates = jax.nn.sigmoid((u_gate.reshape(bsz, s_len, N_BRANCH, D_MODEL)
                                + gate_b[l].astype(u_gate.dtype)).astype(jnp.float32)).astype(x.dtype)
        merged = gates[:, :, 0] * y_a + gates[:, :, 1] * y_b + gates[:, :, 2] * y_c
        x = x + merged @ w_o[l]
        x = x + conv_glu_ffn(rms_norm(x, norm2_w[l]), ffn_up_w[l], ffn_dw_w[l], ffn_down_w[l])
    return rms_norm(x, final_norm_w)
```

```python
import math
import os as _os
import numpy as np
from contextlib import ExitStack
import ml_dtypes
import concourse.bass as bass
import concourse.mybir as mybir
from concourse.bass_utils import run_bass_kernel_spmd

F32 = mybir.dt.float32
BF16 = mybir.dt.bfloat16
I32 = mybir.dt.int32
AF = mybir.ActivationFunctionType
ALU = mybir.AluOpType
AX = mybir.AxisListType

NCORE = 8
D = 2048; S = 16384; TC = S // NCORE; TT = 512; NT = TC // TT; KC = 16
DEPTH = 2
O1, O2, O3, O4, O5, O6, O7 = 2048, 4096, 7168, 7200, 8224, 9248, 10272
DFF = 5632; FC = DFF // 128
TWO_PI = 2.0 * math.pi
EX1B = 3072; EX1F = 3104
HALO_H = 32; HALO_X = 8


class Buf:
    __slots__ = ("name", "w", "r", "rp", "dsem", "dcnt", "excl")

    def __init__(self, name="", excl=False):
        self.name = name; self.w = []; self.r = []; self.rp = []; self.dsem = None; self.dcnt = 0
        self.excl = excl


def PBuf():
    return Buf("psum", excl=True)


def _compact(evs):
    best = {}
    for k, v in evs:
        if best.get(k, 0) < v:
            best[k] = v
    return list(best.items())


class Prog:
    ENG = ("pe", "act", "dve", "pool", "sp")

    def __init__(self, nc, stack):
        self.nc = nc; self.stack = stack
        self.eng = {"pe": nc.tensor, "act": nc.scalar, "dve": nc.vector, "pool": nc.gpsimd, "sp": nc.sync}
        self.cnt = {}; self.semobj = {}
        for e in self.ENG:
            self.semobj[e] = stack.enter_context(nc.semaphore("s_" + e)); self.cnt[e] = 0
        self.waited = {e: {} for e in self.ENG}
        self.ndsem = 0; self.dcount = {}
        self.free_keys = []; self.alloc_log = []; self.uid = 0; self.pend = {}

    def _dsem(self, buf):
        if buf.dsem is None:
            if self.free_keys:
                key = self.free_keys.pop()
                buf.dcnt = self.dcount.get(key, 0)
            else:
                key = "d%d" % self.ndsem; self.ndsem += 1
                assert self.ndsem < 92, "too many semaphores"
                self.semobj[key] = self.stack.enter_context(self.nc.semaphore(key))
            self.alloc_log.append(key)
            buf.dsem = key
        return buf.dsem

    def _wait(self, e, ev):
        key, val = ev
        if self.waited[e].get(key, 0) >= val:
            return
        self.waited[e][key] = val
        self.eng[e].wait_ge(self.semobj[key], val)

    def _deps(self, e, reads, writes, pw, accum):
        for b in reads:
            for ev in b.w: self._wait(e, ev)
            if b.excl:
                for ev in b.r:
                    if ev[0] != e: self._wait(e, ev)
        for b in writes:
            if not (accum and len(b.w) == 1 and b.w[0][0] == e):
                for ev in b.w: self._wait(e, ev)
            for ev in b.r: self._wait(e, ev)
        for b in pw:
            for ev in b.r: self._wait(e, ev)
            for ev in b.rp: self._wait(e, ev)

    def _commit(self, ev, reads, writes, pw):
        for b in reads:
            b.r.append(ev)
            if len(b.r) > 10: b.r = _compact(b.r)
        for b in writes:
            b.rp = _compact(b.r + b.w); b.w = [ev]; b.r = []
        for b in pw:
            b.w.append(ev)
            if len(b.w) > 10: b.w = _compact(b.w)

    def op(self, e, fn, reads=(), writes=(), pw=(), accum=False, inc=True):
        self._deps(e, reads, writes, pw, accum)
        ins = fn(self.eng[e])
        if not inc:
            for b in writes:
                b.w = [(e, self.cnt[e] + 1)]; b.r = []
            self.pend.setdefault(e, []).extend(reads)
            return
        self.cnt[e] += 1
        if self.pend.get(e):
            reads = list(reads) + self.pend[e]; self.pend[e] = []
        ins.then_inc(self.semobj[e], 1)
        self._commit((e, self.cnt[e]), reads, writes, pw)

    def _tracked(self, q, track, ins, inc, reads, writes, pw):
        key = self._dsem(track)
        track.dcnt += inc; self.dcount[key] = track.dcnt
        ins.then_inc(self.semobj[key], inc)
        self._commit((key, track.dcnt), reads, writes, pw)

    def dma(self, q, out, in_, reads=(), writes=(), pw=(), track=None):
        self._deps(q, reads, writes, pw, False)
        if track is None:
            track = writes[0] if writes else (pw[0] if pw else reads[0])
        ins = self.eng[q].dma_start(out=out, in_=in_)
        self._tracked(q, track, ins, 16, reads, writes, pw)

    def gather(self, out, in_rows, idx_ap, reads=(), writes=(), pw=(), track=None):
        self._deps("pool", reads, writes, pw, False)
        if track is None:
            track = writes[0] if writes else pw[0]
        ins = self.nc.gpsimd.indirect_dma_start(out=out, out_offset=None, in_=in_rows,
                                                in_offset=bass.IndirectOffsetOnAxis(ap=idx_ap, axis=0))
        self._tracked("pool", track, ins, 16, reads, writes, pw)

    def allgather(self, out_ap, in_ap, reads, writes, track):
        self._deps("pool", reads, writes, (), False)
        ins = self.nc.gpsimd.collective_compute("AllGather", ALU.bypass, replica_groups=[list(range(NCORE))],
                                                ins=[in_ap.opt()], outs=[out_ap.opt()])
        self._tracked("pool", track, ins, 1, reads, writes, ())

    def finish(self, bufs, e="sp"):
        for b in bufs:
            for ev in b.w + b.r: self._wait(e, ev)

    def barrier(self):
        evs = [(e, self.cnt[e]) for e in self.ENG if self.cnt[e] > 0] + list(self.dcount.items())
        for e in self.ENG:
            for ev in evs: self._wait(e, ev)


class Scope:
    def __init__(self, p):
        self.p = p; self.st = ExitStack()

    def __enter__(self):
        self.st.__enter__(); self.mark = len(self.p.alloc_log); return self

    def __exit__(self, *a):
        self.p.barrier()
        for key in self.p.alloc_log[self.mark:]:
            if key not in self.p.free_keys:
                self.p.free_keys.append(key)
        del self.p.alloc_log[self.mark:]
        return self.st.__exit__(*a)

    def _nm(self, name):
        self.p.uid += 1
        return "%s_%d" % (name, self.p.uid)

    def sb(self, name, shape, dt=F32):
        return self.st.enter_context(self.p.nc.sbuf_tensor(self._nm(name), list(shape), dt))

    def ps(self, name, shape, dt=F32):
        return self.st.enter_context(self.p.nc.psum_tensor(self._nm(name), list(shape), dt))


def a_chunks():
    ch = []; ar = np.arange(128)
    for i in range(8):
        ch.append(("g", i, 1024 + i * 128 + ar)); ch.append(("a", i, i * 128 + ar))
    for i in range(16):
        ch.append(("z", i, O1 + i * 128 + ar))
    for i in range(24):
        ch.append(("xbc", i, O2 + i * 128 + ar))
    dtc = np.full(128, -1); dtc[:32] = O3 + np.arange(32)
    ch.append(("dt", 0, dtc))
    for base, nm in ((O4, "q"), (O5, "k")):
        for h in range(8):
            cols = base + h * 128 + ar
            sw = np.full(128, -1)
            for r in range(128):
                d = r % 64; b = base + h * 128 + (r // 64) * 64
                if d < 8: sw[r] = b + d + 8
                elif d < 16: sw[r] = b + d - 8
            ch.append((nm, h, cols)); ch.append((nm + "sw", h, sw))
    for i in range(8):
        ch.append(("v", i, O6 + i * 128 + ar))
    for i in range(48):
        ch.append(("gate", i, O7 + i * 128 + ar))
    return ch


A_CHUNKS = a_chunks()
NCH_A = len(A_CHUNKS)
NCH_A_PAD = 152


def lay_cols(W, cols):
    K = W.shape[0]
    sub = np.zeros((K, 128), np.float32)
    m = cols >= 0
    sub[:, m] = W[:, cols[m]]
    return sub.reshape(K // 128, 128, 128).transpose(1, 0, 2).reshape(128, -1)


def lay_linear(W, order=None):
    K, N = W.shape
    n = N // 128
    out = W.reshape(K // 128, 128, n, 128).transpose(2, 1, 0, 3).reshape(n, 128, K)
    if order is not None:
        out = out[order]
    return np.ascontiguousarray(out)


def pvec(v):
    return np.ascontiguousarray(v.reshape(-1, 128).T)


WSPEC = [("win", NCH_A_PAD, KC * 128), ("wco", 16, 8 * 128), ("wso", 16, 16 * 128), ("wdo", 16, 8 * 128),
         ("wo", 16, 16 * 128), ("wup", 2 * FC, 16 * 128), ("wdn", 16, FC * 128)]

PSPEC = [("n1w", 16), ("gb", 48), ("cdw", 8 * 31), ("cdb", 8), ("lnw", 8), ("lnb", 8),
         ("scw", 16), ("scb", 4), ("dtb", 4), ("alog", 4), ("dsk", 4), ("snw", 16),
         ("lq1", 64), ("lk1", 64), ("lq2", 64), ("lk2", 64), ("subw", 1), ("linit", 1), ("oml", 1),
         ("n2w", 16), ("fdw", FC * 3), ("fnw", 16), ("hflag", 1), ("ropec", 2)]
POFF = {}
_o = 0
for _n, _w in PSPEC:
    POFF[_n] = (_o, _w); _o += _w
NPAR = _o
CSPEC = [("tri", 128), ("trimask", 128), ("ident", 128), ("sel", 128)]

IDX_QKV = 0; IDX_SSD = 24; IDX_C = 88; IDX_HH = 120; IDX_HX = 128; NIDX = 144


def build_idx(core):
    j = core; g = core // 2; p = np.arange(128)
    idx = np.zeros((128, NIDX), np.int32)
    for r in range(8):
        for t in range(3):
            idx[:, IDX_QKV + r * 3 + t] = r * EX1B + t * 1024 + j * 128 + p
        for hf in range(2):
            o = IDX_SSD + (r * 2 + hf) * 4
            idx[:, o + 0] = (r * EX1F + j * 256 + p) * 2 + hf
            idx[:, o + 1] = (r * EX1F + j * 256 + 128 + p) * 2 + hf
            idx[:, o + 2] = (r * EX1F + 2048 + g * 128 + p) * 2 + hf
            idx[:, o + 3] = (r * EX1F + 2560 + g * 128 + p) * 2 + hf
    for c in range(16):
        for hf in range(2):
            idx[:, IDX_C + c * 2 + hf] = ((c * 128 + p) * 8 + core) * 2 + hf
        idx[:, IDX_HX + c] = max(core - 1, 0) * 2048 + c * 128 + p
    for c in range(8):
        idx[:, IDX_HH + c] = max(core - 1, 0) * 1024 + c * 128 + p
    return idx


FJ = 0; FG = 8; FP = 12; NFLG = 20


def build_flags(core):
    f = np.zeros((128, NFLG), np.float32)
    f[:, FJ + core] = 1.0
    f[:, FG + core // 2] = 1.0
    if core > 0:
        f[:, FP + core - 1] = 1.0
    return f


def build_selb(core):
    sb = np.zeros((128, 8, 128), np.float32)
    sb[:, core, :] = np.eye(128)
    return sb.astype(ml_dtypes.bfloat16)


def build_consts(core):
    c = np.zeros((128, 516), np.float32)
    i = np.arange(128)
    c[:, 0:128] = (i[:, None] <= i[None, :])
    c[:, 128:256] = (i[:, None] <= i[None, :])
    c[:, 256:384] = np.eye(128)
    sel = np.zeros((4, 32), np.float32)
    for hh in range(4):
        sel[hh, core * 4 + hh] = 1.0
    c[:, 384:512] = sel.reshape(1, 128)
    c[0:32, 512:516] = sel.T
    return c


def build_par(inp, l, core):
    par = np.zeros((128, NPAR), np.float32)

    def put(name, arr):
        o, w = POFF[name]
        par[:, o:o + w] = np.asarray(arr, np.float32).reshape(128, w) if np.ndim(arr) == 2 else arr
    j = core; g = core // 2
    put("n1w", pvec(inp["norm1_w"][l]))
    put("gb", pvec(inp["gate_b"][l].reshape(-1)))
    put("cdw", np.ascontiguousarray(inp["conv_dw_w"][l].reshape(31, 8, 128).transpose(2, 1, 0)).reshape(128, 8 * 31))
    put("cdb", pvec(inp["conv_dw_b"][l])); put("lnw", pvec(inp["conv_ln_w"][l])); put("lnb", pvec(inp["conv_ln_b"][l]))
    chs = [np.arange(j * 256, j * 256 + 128), np.arange(j * 256 + 128, j * 256 + 256),
           2048 + g * 128 + np.arange(128), 2560 + g * 128 + np.arange(128)]
    scw = np.stack([inp["ssd_conv_w"][l][:, c].T for c in chs], axis=1)
    put("scw", scw.reshape(128, 16))
    put("scb", np.stack([inp["ssd_conv_b"][l][c] for c in chs], axis=1))
    hs = slice(4 * j, 4 * j + 4)
    put("dtb", np.broadcast_to(inp["ssd_dt_bias"][l][hs], (128, 4)))
    put("alog", np.broadcast_to(inp["ssd_a_log"][l][hs], (128, 4)))
    put("dsk", np.broadcast_to(inp["ssd_d"][l][hs], (128, 4)))
    put("snw", pvec(inp["ssd_norm_w"][l]))
    for nm, key in (("lq1", "da_lambda_q1"), ("lk1", "da_lambda_k1"), ("lq2", "da_lambda_q2"), ("lk2", "da_lambda_k2")):
        put(nm, np.broadcast_to(inp[key][l], (128, 64)))
    put("subw", inp["da_subln_w"][l].reshape(128, 1))
    linit = 0.8 - 0.6 * math.exp(-0.3 * l)
    put("linit", np.full((128, 1), linit, np.float32)); put("oml", np.full((128, 1), 1.0 - linit, np.float32))
    put("n2w", pvec(inp["norm2_w"][l]))
    put("fdw", np.ascontiguousarray(inp["ffn_dw_w"][l].reshape(3, FC, 128).transpose(2, 1, 0)).reshape(128, FC * 3))
    put("fnw", pvec(inp["final_norm_w"]))
    put("hflag", np.full((128, 1), 0.0 if core == 0 else 1.0, np.float32))
    inv = 1.0 / (500000.0 ** (np.arange(0, 16, 2, dtype=np.float32) / 16.0))
    rc = np.zeros((128, 2), np.float32)
    for r in range(128):
        d = r % 64
        if d < 16:
            rc[r, 0] = inv[d % 8]; rc[r, 1] = -1.0 if d < 8 else 1.0
    put("ropec", rc)
    return par


class Ctx:
    pass


def emit_rope_tables(p, cx, pos_ap, Ct, bC, St, bS):
    with Scope(p) as s:
        n = TC
        posi = s.sb("rp_posi", [128, n], I32); bpi = Buf()
        ang = s.sb("rp_ang", [128, n]); bang = Buf()
        t1 = s.sb("rp_t1", [128, n]); b1 = Buf()
        ki = s.sb("rp_ki", [128, n], I32); bki = Buf()
        t2 = s.sb("rp_t2", [128, n]); b2 = Buf()
        t3 = s.sb("rp_t3", [128, n]); b3 = Buf()
        rc = cx.P(0, "ropec")
        p.dma("sp", posi[:], pos_ap.partition_broadcast(128), writes=[bpi])
        p.op("dve", lambda e: e.tensor_copy(out=t1[:], in_=posi[:]), reads=[bpi], writes=[b1])
        p.op("dve", lambda e: e.tensor_scalar(out=ang[:], in0=t1[:], scalar1=rc[:, 0:1], scalar2=None, op0=ALU.mult),
             reads=[b1, cx.bpar[0]], writes=[bang])
        for which, (dst, bd) in enumerate(((St, bS), (Ct, bC))):
            shift = 0.0 if which == 0 else math.pi / 2
            p.op("dve", lambda e: e.tensor_scalar(out=t1[:], in0=ang[:], scalar1=shift, scalar2=1.0 / TWO_PI, op0=ALU.add, op1=ALU.mult),
                 reads=[bang], writes=[b1])
            p.op("dve", lambda e: e.tensor_copy(out=ki[:], in_=t1[:]), reads=[b1], writes=[bki])
            p.op("dve", lambda e: e.tensor_copy(out=t2[:], in_=ki[:]), reads=[bki], writes=[b2])
            p.op("dve", lambda e: e.scalar_tensor_tensor(out=t3[:], in0=t2[:], scalar=-TWO_PI, in1=ang[:], op0=ALU.mult, op1=ALU.add),
                 reads=[b2, bang], writes=[b3])
            if shift != 0.0:
                p.op("dve", lambda e: e.tensor_scalar(out=t3[:], in0=t3[:], scalar1=shift, scalar2=None, op0=ALU.add),
                     reads=[b3], writes=[b3])
            p.op("dve", lambda e: e.tensor_scalar(out=t1[:], in0=t3[:], scalar1=math.pi, scalar2=-TWO_PI, op0=ALU.is_gt, op1=ALU.mult),
                 reads=[b3], writes=[b1])
            p.op("dve", lambda e: e.tensor_tensor(out=t3[:], in0=t3[:], in1=t1[:], op=ALU.add), reads=[b3, b1], writes=[b3])
            p.op("dve", lambda e: e.tensor_scalar(out=t1[:], in0=t3[:], scalar1=-math.pi, scalar2=TWO_PI, op0=ALU.is_lt, op1=ALU.mult),
                 reads=[b3], writes=[b1])
            p.op("dve", lambda e: e.tensor_tensor(out=t3[:], in0=t3[:], in1=t1[:], op=ALU.add), reads=[b3, b1], writes=[b3])
            p.op("dve", lambda e: e.tensor_scalar(out=t3[:], in0=t3[:], scalar1=math.pi, scalar2=-math.pi, op0=ALU.min, op1=ALU.max),
                 reads=[b3], writes=[b3])
            if which == 0:
                p.op("act", lambda e: e.activation(out=t2[:], in_=t3[:], func=AF.Sin), reads=[b3], writes=[b2])
                p.op("dve", lambda e: e.tensor_scalar(out=dst[:], in0=t2[:], scalar1=rc[:, 1:2], scalar2=None, op0=ALU.mult),
                     reads=[b2, cx.bpar[0]], writes=[bd])
            else:
                p.op("act", lambda e: e.activation(out=dst[:], in_=t3[:], func=AF.Sin), reads=[b3], writes=[bd])


class WStream:
    def __init__(self, p, s, name, kw, nslots=3):
        self.p = p; self.kw = kw; self.n = nslots
        self.sl = [s.sb("%s%d" % (name, i), [128, kw], BF16) for i in range(nslots)]
        self.b = [Buf("%s%d" % (name, i)) for i in range(nslots)]
        self.q = []; self.i = 0

    def prefetch(self, src_ap, bsrc):
        k = self.i % self.n; self.i += 1
        self.p.dma("pool", self.sl[k][:], src_ap, reads=[bsrc], writes=[self.b[k]])
        self.q.append(k)

    def pop(self):
        k = self.q.pop(0)
        return self.sl[k], self.b[k]


def phase_A(p, cx, l, xsrc, bxsrc):
    nc = p.nc
    with Scope(p) as s:
        xn = s.sb("xn", [128, KC, TC], BF16)
        NTA = TC // 256
        bxn = [Buf("xn%d" % t) for t in range(NT)]
        n1 = cx.P(l, "n1w"); gbs = cx.P(l, "gb"); bpar = cx.bpar[l]
        Ct = s.sb("Ct", [128, TC]); bC = Buf(); St = s.sb("St", [128, TC]); bS = Buf()
        emit_rope_tables(p, cx, cx.pos, Ct, bC, St, bS)
        with Scope(p) as s1:
            TQ = 256
            xt = [s1.sb("xt%d" % i, [128, KC, TQ]) for i in range(2)]; bxt = [Buf() for _ in range(2)]
            sq = s1.sb("sq", [128, KC, TQ]); bsq = Buf()
            rstd = s1.sb("rstd", [128, TQ]); brs = Buf()
            pss = s1.ps("pss", [128, TQ]); bpss = Buf()
            xv = xsrc.rearrange("(kc p) t -> p kc t", p=128)
            for tq_ in range(NTA):
                k = tq_ % 2
                p.dma("sp", xt[k][:], xv[:, :, tq_ * TQ:(tq_ + 1) * TQ], reads=[bxsrc], writes=[bxt[k]])
                p.op("act", lambda e: e.activation(out=sq[:], in_=xt[k][:], func=AF.Square), reads=[bxt[k]], writes=[bsq])
                for kc in range(KC):
                    p.op("pe", lambda e: e.matmul(pss[:], cx.ones[:], sq[:, kc, :], start=(kc == 0), stop=(kc == KC - 1)),
                         reads=[cx.bconst, bsq], writes=[bpss], accum=True, inc=(kc == KC - 1))
                p.op("act", lambda e: e.activation(out=rstd[:], in_=pss[:], func=AF.Ln, scale=1.0 / D, bias=cx.eps6[:, 0:1]),
                     reads=[bpss, cx.bconst], writes=[brs])
                p.op("act", lambda e: e.activation(out=rstd[:], in_=rstd[:], func=AF.Exp, scale=-0.5), reads=[brs], writes=[brs])
                for kc in range(KC):
                    p.op("dve", lambda e: e.scalar_tensor_tensor(out=xn[:, kc, tq_ * TQ:(tq_ + 1) * TQ], in0=xt[k][:, kc, :],
                                                                 scalar=n1[:, kc:kc + 1], in1=rstd[:], op0=ALU.mult, op1=ALU.mult),
                         reads=[bxt[k], bpar, brs], pw=[bxn[tq_ * TQ // TT]])
        ws = WStream(p, s, "wA", KC * 128)
        NPS = 6
        pst = [s.ps("pm%d" % i, [128, TT]) for i in range(NPS)]; bps = [Buf() for _ in range(NPS)]
        NSF = 4
        stF = [s.sb("stF%d" % i, [128, TT]) for i in range(NSF)]; bsF = [Buf() for _ in range(NSF)]
        NSB = 3
        stB = [s.sb("stB%d" % i, [128, TT], BF16) for i in range(NSB)]; bsB = [Buf() for _ in range(NSB)]
        tmp = [s.sb("tmp%d" % i, [128, NT, TT]) for i in range(2)]
        btmp = [[Buf() for _ in range(NT)] for _ in range(2)]
        tmp2 = s.sb("tmp2", [128, TT]); btmp2 = Buf()
        cnt = {"ps": 0, "sf": 0, "sb": 0, "tmp": 0}
        win = cx.W[l]["win"]; bwin = cx.bW[l]["win"]

        def store(dst_ap, bdst, src, bsrc):
            p.dma("sp", dst_ap, src, reads=[bsrc], pw=[bdst], track=bsrc)

        LOOK = 2
        for ci in range(LOOK):
            ws.prefetch(win[ci], bwin)
        cur_tmp = 0
        for ci, (kind, idx, _) in enumerate(A_CHUNKS):
            if ci + LOOK < NCH_A:
                ws.prefetch(win[ci + LOOK], bwin)
            wt, bw = ws.pop()
            wv = wt[:].rearrange("p (k j) -> p k j", j=128)
            if kind in ("g", "q", "k"):
                cur_tmp = cnt["tmp"] % 2; cnt["tmp"] += 1
            for t in range(NT):
                pi = cnt["ps"] % NPS; cnt["ps"] += 1
                ps = pst[pi]; bp = bps[pi]
                for kc in range(KC):
                    p.op("pe", lambda e: e.matmul(ps[:], wv[:, kc, :], xn[:, kc, t * TT:(t + 1) * TT],
                                                  start=(kc == 0), stop=(kc == KC - 1)),
                         reads=[bw, bxn[t]], writes=[bp], accum=True, inc=(kc == KC - 1))
                tsl = slice(t * TT, (t + 1) * TT)
                rows = slice(idx * 128, (idx + 1) * 128)

                def fslot():
                    fi = cnt["sf"] % NSF; cnt["sf"] += 1
                    return stF[fi], bsF[fi]

                def bslot():
                    bi = cnt["sb"] % NSB; cnt["sb"] += 1
                    return stB[bi], bsB[bi]
                if kind == "g":
                    p.op("act", lambda e: e.activation(out=tmp[cur_tmp][:, t, :], in_=ps[:], func=AF.Sigmoid),
                         reads=[bp], writes=[btmp[cur_tmp][t]])
                elif kind == "a":
                    sf, bf = fslot()
                    p.op("dve", lambda e: e.tensor_tensor(out=sf[:], in0=ps[:], in1=tmp[cur_tmp][:, t, :], op=ALU.mult),
                         reads=[bp, btmp[cur_tmp][t]], writes=[bf])
                    store(cx.hT[rows, tsl], cx.bhT, sf[:], bf)
                    if t == NT - 1:
                        store(cx.exh.rearrange("(q c) w -> q c w", c=8)[:, idx, :], cx.bexh, sf[:, TT - HALO_H:TT], bf)
                elif kind == "z":
                    sf, bf = fslot()
                    p.op("act", lambda e: e.activation(out=sf[:], in_=ps[:], func=AF.Silu), reads=[bp], writes=[bf])
                    store(cx.szT[rows, tsl], cx.bszT, sf[:], bf)
                elif kind == "xbc":
                    sf, bf = fslot()
                    p.op("dve", lambda e: e.tensor_copy(out=sf[:], in_=ps[:]), reads=[bp], writes=[bf])
                    store(cx.ex1f[rows, tsl], cx.bex1f, sf[:], bf)
                elif kind == "dt":
                    sf, bf = fslot()
                    p.op("dve", lambda e: e.tensor_copy(out=sf[:], in_=ps[:]), reads=[bp], writes=[bf])
                    store(cx.ex1f[3072:3104, tsl], cx.bex1f, sf[0:32, :], bf)
                elif kind in ("q", "k"):
                    p.op("dve", lambda e: e.tensor_tensor(out=tmp[cur_tmp][:, t, :], in0=ps[:], in1=Ct[:, tsl], op=ALU.mult),
                         reads=[bp, bC], writes=[btmp[cur_tmp][t]])
                elif kind in ("qsw", "ksw"):
                    sb_, bb = bslot()
                    p.op("dve", lambda e: e.tensor_tensor(out=tmp2[:], in0=ps[:], in1=St[:, tsl], op=ALU.mult),
                         reads=[bp, bS], writes=[btmp2])
                    p.op("dve", lambda e: e.tensor_tensor(out=sb_[:], in0=tmp2[:], in1=tmp[cur_tmp][:, t, :], op=ALU.add),
                         reads=[btmp2, btmp[cur_tmp][t]], writes=[bb])
                    base = 0 if kind == "qsw" else 1024
                    store(cx.ex1b[base + idx * 128:base + (idx + 1) * 128, tsl], cx.bex1b, sb_[:], bb)
                elif kind == "v":
                    sb_, bb = bslot()
                    p.op("act", lambda e: e.activation(out=sb_[:], in_=ps[:], func=AF.Copy), reads=[bp], writes=[bb])
                    store(cx.ex1b[2048 + idx * 128:2048 + (idx + 1) * 128, tsl], cx.bex1b, sb_[:], bb)
                elif kind == "gate":
                    sf, bf = fslot()
                    p.op("act", lambda e: e.activation(out=sf[:], in_=ps[:], func=AF.Sigmoid, bias=gbs[:, idx:idx + 1]),
                         reads=[bp, bpar], writes=[bf])
                    store(cx.gT[rows, tsl], cx.bgT, sf[:], bf)
    p.allgather(cx.g1b, cx.ex1b, reads=[cx.bex1b], writes=[cx.bg1b], track=cx.bg1b)
    p.allgather(cx.g1f, cx.ex1f, reads=[cx.bex1f], writes=[cx.bg1f], track=cx.bg1f)
    p.allgather(cx.gh, cx.exh, reads=[cx.bexh], writes=[cx.bgh], track=cx.bgh)


def phase_B1(p, cx, l):
    bpar = cx.bpar[l]
    NQT = S // TT; NKT = S // 128
    with Scope(p) as s:
        qT = s.sb("qT", [128, S], BF16); bq = Buf("qT")
        kT = s.sb("kT", [128, S], BF16); bk = Buf("kT")
        Vtm = s.sb("Vtm", [128, NKT, 128], BF16); bV = Buf("Vtm")
        onesb = s.sb("onesb", [128, 128], BF16); bob = Buf()
        p.op("dve", lambda e: e.memset(onesb[:], 1.0), writes=[bob])
        lam = s.sb("lam", [128, 4]); blam = Buf()
        tl = s.sb("tl", [128, 64]); btl = Buf()
        for i, (a, b) in enumerate((("lq1", "lk1"), ("lq2", "lk2"))):
            p.op("dve", lambda e: e.tensor_tensor(out=tl[:], in0=cx.P(l, a), in1=cx.P(l, b), op=ALU.mult), reads=[bpar], writes=[btl])
            p.op("dve", lambda e: e.tensor_reduce(out=lam[:, i:i + 1], in_=tl[:], axis=AX.X, op=ALU.add), reads=[btl], writes=[blam])
        p.op("act", lambda e: e.activation(out=lam[:, 0:2], in_=lam[:, 0:2], func=AF.Exp), reads=[blam], writes=[blam])
        p.op("dve", lambda e: e.tensor_tensor(out=lam[:, 2:3], in0=lam[:, 1:2], in1=lam[:, 0:1], op=ALU.subtract), reads=[blam], writes=[blam])
        p.op("dve", lambda e: e.tensor_tensor(out=lam[:, 2:3], in0=lam[:, 2:3], in1=cx.P(l, "linit"), op=ALU.subtract), reads=[blam, bpar], writes=[blam])
        p.op("dve", lambda e: e.tensor_tensor(out=lam[:, 3:4], in0=cx.P(l, "subw"), in1=cx.P(l, "oml"), op=ALU.mult), reads=[bpar], writes=[blam])
        with Scope(p) as s1:
            cand = [s1.sb("cand%d" % i, [128, 8, TT], BF16) for i in range(3)]; bcand = [Buf() for _ in range(3)]
            psel = [s1.ps("psel%d" % i, [128, TT]) for i in range(2)]; bpsel = [Buf() for _ in range(2)]
            pv = [s1.ps("pv%d" % i, [128, 4, 128]) for i in range(2)]; bpv = [Buf() for _ in range(2)]
            n = 0; m = 0; g = 0
            for r in range(8):
                for tile in range(NT):
                    for tsel in range(3):
                        k = n % 3; n += 1
                        src = cx.g1b[r * EX1B + tsel * 1024:r * EX1B + (tsel + 1) * 1024, tile * TT:(tile + 1) * TT].rearrange("(c q) t -> q c t", q=128)
                        p.dma("sp", cand[k][:], src, reads=[cx.bg1b], writes=[bcand[k]])
                        if tsel < 2:
                            a_ = m % 2; m += 1
                            for c in range(8):
                                p.op("pe", lambda e: e.matmul(psel[a_][:], cx.selB[:, c, :], cand[k][:, c, :], start=(c == 0), stop=(c == 7)),
                                     reads=[cx.bconst, bcand[k]], writes=[bpsel[a_]], accum=True, inc=(c == 7))
                            dst, bd = (qT, bq) if tsel == 0 else (kT, bk)
                            cols = slice(r * TC + tile * TT, r * TC + (tile + 1) * TT)
                            if tsel == 0:
                                p.op("act", lambda e: e.activation(out=dst[:, cols], in_=psel[a_][:], func=AF.Copy), reads=[bpsel[a_]], pw=[bd])
                            else:
                                p.op("dve", lambda e: e.tensor_copy(out=dst[:, cols], in_=psel[a_][:]), reads=[bpsel[a_]], pw=[bd])
                        else:
                            b_ = g % 2; g += 1
                            for u in range(4):
                                for c in range(8):
                                    p.op("pe", lambda e: e.matmul(pv[b_][:, u, :], cand[k][:, c, u * 128:(u + 1) * 128], cx.selB[:, c, :], start=(c == 0), stop=(c == 7)),
                                         reads=[cx.bconst, bcand[k]], pw=[bpv[b_]] if u else (), writes=[bpv[b_]] if u == 0 else (), accum=True, inc=(c == 7))
                            kt0 = (r * TC + tile * TT) // 128
                            p.op("dve" if b_ else "act",
                                 (lambda e: e.tensor_copy(out=Vtm[:, kt0:kt0 + 4, :], in_=pv[b_][:])) if b_ else
                                 (lambda e: e.activation(out=Vtm[:, kt0:kt0 + 4, :], in_=pv[b_][:], func=AF.Copy)),
                                 reads=[bpv[b_]], pw=[bV])
        pS = [s.ps("pS%d" % i, [128, TT]) for i in range(2)]; bpS = [Buf() for _ in range(2)]
        pO = [s.ps("pO%d" % i, [128, TT]) for i in range(2)]; bpO = [Buf() for _ in range(2)]
        pSum = [s.ps("pSum%d" % i, [128, TT]) for i in range(2)]; bpSum = [Buf() for _ in range(2)]
        pN = s.ps("pN", [128, TT]); bpN = Buf()
        NPT = 3
        pt = [s.sb("pt%d" % i, [128, TT], BF16) for i in range(NPT)]; bpt = [Buf() for _ in range(NPT)]
        oc = [s.sb("oc%d" % i, [128, TT]) for i in range(2)]; boc = [Buf() for _ in range(2)]
        rec = s.sb("rec", [128, TT]); brec = Buf()
        osq = s.sb("osq", [128, TT]); bosq = Buf()
        rs = s.sb("rsn", [128, TT]); brs = Buf()
        ost = [s.sb("ost%d" % i, [128, TT], BF16) for i in range(2)]; bost = [Buf() for _ in range(2)]
        tc = 0
        for i in range(NQT):
            qs = slice(i * TT, (i + 1) * TT)
            nk = 4 * (i + 1)
            for c in range(2):
                rws = slice(64 * c, 64 * c + 64)
                for j in range(nk):
                    a = tc % 2; m = tc % NPT; tc += 1
                    p.op("pe", lambda e: e.matmul(pS[a][:], kT[rws, j * 128:(j + 1) * 128], qT[rws, qs], start=True, stop=True),
                         reads=[bk, bq], writes=[bpS[a]])
                    p.op("act", lambda e: e.activation(out=pt[m][:], in_=pS[a][:], func=AF.Exp, scale=0.125),
                         reads=[bpS[a]], writes=[bpt[m]])
                    if j >= 4 * i:
                        p.op("pool", lambda e: e.affine_select(out=pt[m][:], in_=pt[m][:], pattern=[[1, TT]], compare_op=ALU.is_ge,
                                                               fill=0.0, base=TT * i - 128 * j, channel_multiplier=-1),
                             reads=[bpt[m]], writes=[bpt[m]])
                    p.op("pe", lambda e: e.matmul(pO[c][:], Vtm[:, j, :], pt[m][:], start=(j == 0), stop=(j == nk - 1)),
                         reads=[bV, bpt[m]], writes=[bpO[c]], accum=True)
                    p.op("pe", lambda e: e.matmul(pSum[c][:], onesb[:], pt[m][:], start=(j == 0), stop=(j == nk - 1)),
                         reads=[bob, bpt[m]], writes=[bpSum[c]], accum=True)
                p.op("dve", lambda e: e.reciprocal(out=rec[:], in_=pSum[c][:]), reads=[bpSum[c]], writes=[brec])
                p.op("dve", lambda e: e.tensor_tensor(out=oc[c][:], in0=pO[c][:], in1=rec[:], op=ALU.mult),
                     reads=[bpO[c], brec], writes=[boc[c]])
            p.op("dve", lambda e: e.scalar_tensor_tensor(out=oc[0][:], in0=oc[1][:], scalar=lam[:, 2:3], in1=oc[0][:], op0=ALU.mult, op1=ALU.add),
                 reads=[boc[0], boc[1], blam], writes=[boc[0]])
            p.op("act", lambda e: e.activation(out=osq[:], in_=oc[0][:], func=AF.Square), reads=[boc[0]], writes=[bosq])
            p.op("pe", lambda e: e.matmul(pN[:], cx.ones[:], osq[:], start=True, stop=True), reads=[cx.bconst, bosq], writes=[bpN])
            p.op("act", lambda e: e.activation(out=rs[:], in_=pN[:], func=AF.Ln, scale=1.0 / 128, bias=cx.eps5[:, 0:1]),
                 reads=[bpN, cx.bconst], writes=[brs])
            p.op("act", lambda e: e.activation(out=rs[:], in_=rs[:], func=AF.Exp, scale=-0.5), reads=[brs], writes=[brs])
            k = i % 2
            p.op("dve", lambda e: e.scalar_tensor_tensor(out=ost[k][:], in0=oc[0][:], scalar=lam[:, 3:4], in1=rs[:], op0=ALU.mult, op1=ALU.mult),
                 reads=[boc[0], blam, brs], writes=[bost[k]])
            p.dma("sp", cx.ex2b[:, qs], ost[k][:], reads=[bost[k]], pw=[cx.bex2b], track=bost[k])
    p.allgather(cx.g2b, cx.ex2b, reads=[cx.bex2b], writes=[cx.bg2b], track=cx.bg2b)


def phase_B2(p, cx, l):
    bpar = cx.bpar[l]
    SC = 1024; NSC = S // SC; CPS = SC // 128
    scw = cx.P(l, "scw"); scb = cx.P(l, "scb")
    with Scope(p) as s:
        raw = [s.sb("raw%d" % i, [128, 4, 3 + SC]) for i in range(2)]; braw = [Buf() for _ in range(2)]
        acc = s.sb("acc", [128, 4, SC]); bacc = [Buf() for _ in range(4)]
        xc = s.sb("xc", [128, 2, SC]); bxc = Buf()
        BT = s.sb("BT", [128, SC], BF16); bBT = Buf()
        CT = s.sb("CT", [128, SC], BF16); bCT = Buf()
        dtr = s.sb("dtr", [128, 1, SC]); bdtr = Buf()
        dtm = s.sb("dtm", [128, CPS, 4]); bdtm = Buf()
        dta = s.sb("dta", [128, CPS, 4]); bdta = Buf()
        tq = [s.sb("tq%d" % i, [128, CPS, 4]) for i in range(3)]; btq = [Buf() for _ in range(3)]
        cumcol = s.sb("cumcol", [128, CPS, 4]); bcc = Buf()
        ecc = s.sb("ecc", [128, CPS, 4]); becc = Buf()
        dend = s.sb("dend", [128, CPS, 4]); bdend = Buf()
        eend = s.sb("eend", [128, CPS, 4]); beend = Buf()
        aneg = s.sb("aneg", [128, 4]); baneg = Buf()
        st = s.sb("st", [128, 256]); bst = Buf()
        stb = s.sb("stb", [128, 256], BF16); bstb = Buf()
        ybuf = [s.sb("ybuf%d" % i, [128, 2, SC]) for i in range(2)]; bybuf = [Buf() for _ in range(2)]
        Xb = [s.sb("Xb%d" % i, [128, 256], BF16) for i in range(2)]; bXb = [Buf() for _ in range(2)]
        Xd = [s.sb("Xd%d" % i, [128, 256], BF16) for i in range(2)]; bXd = [Buf() for _ in range(2)]
        xD = [s.sb("xD%d" % i, [128, 256]) for i in range(2)]; bxD = [Buf() for _ in range(2)]
        Btm = [s.sb("Btm%d" % i, [128, 128], BF16) for i in range(2)]; bBtm = [Buf() for _ in range(2)]
        CBm = [s.sb("CBm%d" % i, [128, 128]) for i in range(2)]; bCBm = [Buf() for _ in range(2)]
        R = [s.sb("R%d" % i, [128, 4, 128]) for i in range(2)]; bR = [Buf() for _ in range(2)]
        seg = [s.sb("seg%d" % i, [128, 4, 128]) for i in range(2)]; bseg = [Buf() for _ in range(2)]
        Wm = [s.sb("Wm%d" % i, [128, 4, 128], BF16) for i in range(2)]; bWm = [Buf() for _ in range(2)]
        yo = [s.sb("yo%d" % i, [128, 256]) for i in range(2)]; byo = [Buf() for _ in range(2)]
        pXT_ = s.ps("pXT", [128, 512]); pXT = pXT_[:, 0:256]; bpXT = PBuf()
        pBt_ = s.ps("pBt", [128, 1024], BF16); pBt = pBt_[:, 0:128]; bpBt = PBuf()
        pCB_ = s.ps("pCB", [128, 512]); pCB = pCB_[:, 0:128]; bpCB = PBuf()
        pcT = s.ps("pcT", [128, 4, 128]); bpcT = PBuf()
        pY_ = s.ps("pY", [128, 512]); pY = pY_[:, 0:256]; bpY = PBuf()
        pOf_ = s.ps("pOf", [128, 512]); pOf = pOf_[:, 0:256]; bpOf = PBuf()
        pYT_ = s.ps("pYT", [128, 4, 128]); pYT = pYT_[:, 0:2, :]; bpYT = PBuf()
        pSt_ = s.ps("pSt", [128, 512]); pSt = pSt_[:, 0:256]; bpSt = PBuf()
        p.op("act", lambda e: e.activation(out=aneg[:], in_=cx.P(l, "alog"), func=AF.Exp), reads=[bpar], writes=[baneg])
        p.op("dve", lambda e: e.tensor_scalar(out=aneg[:], in0=aneg[:], scalar1=-1.0, scalar2=None, op0=ALU.mult), reads=[baneg], writes=[baneg])
        p.op("dve", lambda e: e.memset(dtr[:], 0.0), writes=[bdtr])
        p.op("dve", lambda e: e.memset(st[:], 0.0), writes=[bst])
        p.op("dve", lambda e: e.memset(stb[:], 0.0), writes=[bstb])
        selv = cx.sel[:].rearrange("p (a h) -> p a h", a=4)

        cnd = [s.sb("cnd%d" % i, [128, SC]) for i in range(3)]; bcnd = [Buf() for _ in range(3)]
        ncnd = [0]

        def load_raw(sc):
            k = sc % 2
            r0 = sc // 2; c0 = (sc % 2) * SC
            base = r0 * EX1F
            cands = [[(base + (2 * c) * 128, FJ + c) for c in range(8)], [(base + (2 * c + 1) * 128, FJ + c) for c in range(8)],
                     [(base + 2048 + c * 128, FG + c) for c in range(4)], [(base + 2560 + c * 128, FG + c) for c in range(4)]]
            for pc in range(4):
                if sc == 0:
                    p.op("dve", lambda e: e.memset(raw[k][:, pc, 0:3], 0.0), pw=[braw[k]])
                else:
                    p.op("dve", lambda e: e.tensor_copy(out=raw[k][:, pc, 0:3], in_=raw[1 - k][:, pc, SC:SC + 3]), reads=[braw[1 - k]], pw=[braw[k]])
                for i_, (row0, fc) in enumerate(cands[pc]):
                    k3 = ncnd[0] % 3; ncnd[0] += 1
                    p.dma("sp", cnd[k3][:], cx.g1f[row0:row0 + 128, c0:c0 + SC], reads=[cx.bg1f], writes=[bcnd[k3]])
                    if i_ == 0:
                        p.op("dve", lambda e: e.tensor_scalar(out=raw[k][:, pc, 3:3 + SC], in0=cnd[k3][:], scalar1=cx.flg[:, fc:fc + 1], scalar2=None, op0=ALU.mult),
                             reads=[bcnd[k3], cx.bconst], pw=[braw[k]])
                    else:
                        p.op("dve", lambda e: e.scalar_tensor_tensor(out=raw[k][:, pc, 3:3 + SC], in0=cnd[k3][:], scalar=cx.flg[:, fc:fc + 1], in1=raw[k][:, pc, 3:3 + SC],
                                                                     op0=ALU.mult, op1=ALU.add), reads=[bcnd[k3], cx.bconst, braw[k]], pw=[braw[k]])

        for sc in range(int(_os.environ.get('B2_NSC', NSC))):
            k = sc % 2
            r0 = sc // 2; c0 = (sc % 2) * SC
            load_raw(sc)
            p.dma("sp", dtr[0:32, 0, :], cx.g1f[r0 * EX1F + 3072:r0 * EX1F + 3104, c0:c0 + SC], reads=[cx.bg1f], pw=[bdtr], track=bdtr)
            for pc in range(4):
                p.op("dve", lambda e: e.tensor_scalar(out=acc[:, pc, :], in0=raw[k][:, pc, 0:SC], scalar1=scw[:, pc * 4:pc * 4 + 1], scalar2=None, op0=ALU.mult),
                     reads=[braw[k], bpar], writes=[bacc[pc]])
                for tap in range(1, 4):
                    p.op("dve", lambda e: e.scalar_tensor_tensor(out=acc[:, pc, :], in0=raw[k][:, pc, tap:tap + SC], scalar=scw[:, pc * 4 + tap:pc * 4 + tap + 1],
                                                                 in1=acc[:, pc, :], op0=ALU.mult, op1=ALU.add),
                         reads=[braw[k], bpar, bacc[pc]], writes=[bacc[pc]])
            for pc in range(2):
                p.op("act", lambda e: e.activation(out=xc[:, pc, :], in_=acc[:, pc, :], func=AF.Silu, bias=scb[:, pc:pc + 1]),
                     reads=[bacc[pc], bpar], pw=[bxc] if pc else (), writes=[bxc] if pc == 0 else ())
            p.op("act", lambda e: e.activation(out=BT[:], in_=acc[:, 2, :], func=AF.Silu, bias=scb[:, 2:3]), reads=[bacc[2], bpar], writes=[bBT])
            p.op("act", lambda e: e.activation(out=CT[:], in_=acc[:, 3, :], func=AF.Silu, bias=scb[:, 3:4]), reads=[bacc[3], bpar], writes=[bCT])
            for ch in range(CPS):
                p.op("pe", lambda e: e.matmul(pXT[:, ch * 4:ch * 4 + 4], dtr[:, 0, ch * 128:(ch + 1) * 128], cx.selT, start=True, stop=True),
                     reads=[bdtr, cx.bconst], pw=[bpXT] if ch else (), writes=[bpXT] if ch == 0 else ())
            p.op("dve", lambda e: e.tensor_copy(out=tq[0][:].rearrange("p c h -> p (c h)"), in_=pXT[:, 0:CPS * 4]), reads=[bpXT], writes=[btq[0]])
            dtb = cx.P(l, "dtb")
            p.op("dve", lambda e: e.tensor_tensor(out=tq[0][:], in0=tq[0][:], in1=dtb.unsqueeze(1).to_broadcast([128, CPS, 4]), op=ALU.add),
                 reads=[btq[0], bpar], writes=[btq[0]])
            p.op("dve", lambda e: e.tensor_scalar(out=tq[1][:], in0=tq[0][:], scalar1=30.0, scalar2=None, op0=ALU.min), reads=[btq[0]], writes=[btq[1]])
            p.op("act", lambda e: e.activation(out=tq[1][:], in_=tq[1][:], func=AF.Exp), reads=[btq[1]], writes=[btq[1]])
            p.op("act", lambda e: e.activation(out=tq[1][:], in_=tq[1][:], func=AF.Ln, bias=cx.one1[:, 0:1]), reads=[btq[1], cx.bconst], writes=[btq[1]])
            p.op("dve", lambda e: e.tensor_tensor(out=dtm[:], in0=tq[0][:], in1=tq[1][:], op=ALU.max), reads=[btq[0], btq[1]], writes=[bdtm])
            p.op("dve", lambda e: e.tensor_tensor(out=dta[:], in0=dtm[:], in1=aneg[:].unsqueeze(1).to_broadcast([128, CPS, 4]), op=ALU.mult),
                 reads=[bdtm, baneg], writes=[bdta])
            p.op("pe", lambda e: e.matmul(pY[:, 0:CPS * 4], cx.tri[:], dta[:].rearrange("p c h -> p (c h)"), start=True, stop=True),
                 reads=[cx.bconst, bdta], writes=[bpY])
            p.op("dve", lambda e: e.tensor_copy(out=cumcol[:].rearrange("p c h -> p (c h)"), in_=pY[:, 0:CPS * 4]), reads=[bpY], writes=[bcc])
            p.op("act", lambda e: e.activation(out=ecc[:], in_=cumcol[:], func=AF.Exp), reads=[bcc], writes=[becc])
            p.op("pe", lambda e: e.matmul(pOf[:, 0:CPS * 4], cx.ones[:], dta[:].rearrange("p c h -> p (c h)"), start=True, stop=True),
                 reads=[cx.bconst, bdta], writes=[bpOf])
            p.op("act", lambda e: e.activation(out=eend[:].rearrange("p c h -> p (c h)"), in_=pOf[:, 0:CPS * 4], func=AF.Exp), reads=[bpOf], writes=[beend])
            p.op("dve", lambda e: e.tensor_tensor(out=tq[2][:].rearrange("p c h -> p (c h)"), in0=pOf[:, 0:CPS * 4], in1=cumcol[:].rearrange("p c h -> p (c h)"), op=ALU.subtract),
                 reads=[bpOf, bcc], writes=[btq[2]])
            p.op("act", lambda e: e.activation(out=tq[2][:], in_=tq[2][:], func=AF.Exp), reads=[btq[2]], writes=[btq[2]])
            p.op("dve", lambda e: e.tensor_tensor(out=dend[:], in0=tq[2][:], in1=dtm[:], op=ALU.mult), reads=[btq[2], bdtm], writes=[bdend])
            yk = sc % 2
            for ch in range(int(_os.environ.get('B2_NCH', CPS))):
                u = ch % 2
                cs = slice(ch * 128, (ch + 1) * 128)
                for hlf in range(2):
                    p.op("pe", lambda e: e.matmul(pXT[:, hlf * 128:(hlf + 1) * 128], xc[:, hlf, cs], cx.ident[:], start=True, stop=True),
                         reads=[bxc, cx.bconst], pw=[bpXT] if hlf else (), writes=[bpXT] if hlf == 0 else ())
                p.op("act", lambda e: e.activation(out=Xb[u][:], in_=pXT[:], func=AF.Copy), reads=[bpXT], writes=[bXb[u]])
                p.op("dve", lambda e: e.tensor_tensor(out=Xd[u][:].rearrange("p (h q) -> p h q", h=4), in0=pXT[:].rearrange("p (h q) -> p h q", h=4),
                                                      in1=dend[:, ch, :].unsqueeze(2).to_broadcast([128, 4, 64]), op=ALU.mult),
                     reads=[bpXT, bdend], writes=[bXd[u]])
                p.op("dve", lambda e: e.tensor_tensor(out=xD[u][:].rearrange("p (h q) -> p h q", h=4), in0=pXT[:].rearrange("p (h q) -> p h q", h=4),
                                                      in1=cx.P(l, "dsk").unsqueeze(2).to_broadcast([128, 4, 64]), op=ALU.mult),
                     reads=[bpXT, bpar], writes=[bxD[u]])
                p.op("pe", lambda e: e.transpose(pBt[:], BT[:, cs], cx.identb[:]), reads=[bBT, cx.bconst], writes=[bpBt])
                p.op("act", lambda e: e.activation(out=Btm[u][:], in_=pBt[:], func=AF.Copy), reads=[bpBt], writes=[bBtm[u]])
                p.op("pe", lambda e: e.matmul(pCB[:], BT[:, cs], CT[:, cs], start=True, stop=True), reads=[bBT, bCT], writes=[bpCB])
                p.op("dve", lambda e: e.tensor_tensor(out=CBm[u][:], in0=pCB[:], in1=cx.trimask[:], op=ALU.mult), reads=[bpCB, cx.bconst], writes=[bCBm[u]])
                p.op("dve", lambda e: e.tensor_tensor(out=R[u][:], in0=cx.tri[:].unsqueeze(1).to_broadcast([128, 4, 128]),
                                                       in1=dta[:, ch, :].unsqueeze(2).to_broadcast([128, 4, 128]), op=ALU.mult),
                     reads=[cx.bconst, bdta], writes=[bR[u]])
                p.op("pe", lambda e: e.matmul(pcT[:].rearrange("p h t -> p (h t)"), cx.ones[:], R[u][:].rearrange("p h t -> p (h t)"), start=True, stop=True),
                     reads=[cx.bconst, bR[u]], writes=[bpcT])
                for h in range(4):
                    p.op("dve", lambda e: e.tensor_scalar(out=seg[u][:, h, :], in0=pcT[:, h, :], scalar1=cumcol[:, ch, h:h + 1], scalar2=0.0,
                                                          op0=ALU.subtract, op1=ALU.min),
                         reads=[bpcT, bcc], pw=[bseg[u]] if h else (), writes=[bseg[u]] if h == 0 else ())
                p.op("act", lambda e: e.activation(out=seg[u][:], in_=seg[u][:], func=AF.Exp), reads=[bseg[u]], writes=[bseg[u]])
                for h in range(4):
                    p.op("dve", lambda e: e.scalar_tensor_tensor(out=Wm[u][:, h, :], in0=seg[u][:, h, :], scalar=dtm[:, ch, h:h + 1], in1=CBm[u][:],
                                                                 op0=ALU.mult, op1=ALU.mult),
                         reads=[bseg[u], bdtm, bCBm[u]], pw=[bWm[u]] if h else (), writes=[bWm[u]] if h == 0 else ())
                for h in range(4):
                    p.op("pe", lambda e: e.matmul(pY[:, h * 64:(h + 1) * 64], Wm[u][:, h, :], Xb[u][:, h * 64:(h + 1) * 64], start=True, stop=True),
                         reads=[bWm[u], bXb[u]], pw=[bpY] if h else (), writes=[bpY] if h == 0 else ())
                p.op("pe", lambda e: e.matmul(pOf[:], CT[:, cs], stb[:], start=True, stop=True), reads=[bCT, bstb], writes=[bpOf])
                p.op("dve", lambda e: e.tensor_tensor(out=yo[u][:].rearrange("p (h q) -> p h q", h=4), in0=pOf[:].rearrange("p (h q) -> p h q", h=4),
                                                      in1=ecc[:, ch, :].unsqueeze(2).to_broadcast([128, 4, 64]), op=ALU.mult),
                     reads=[bpOf, becc], writes=[byo[u]])
                p.op("dve", lambda e: e.tensor_tensor(out=yo[u][:], in0=yo[u][:], in1=pY[:], op=ALU.add), reads=[byo[u], bpY], writes=[byo[u]])
                p.op("pool", lambda e: e.tensor_tensor(out=yo[u][:], in0=yo[u][:], in1=xD[u][:], op=ALU.add), reads=[byo[u], bxD[u]], writes=[byo[u]])
                for hlf in range(2):
                    p.op("pe", lambda e: e.matmul(pYT[:, hlf, :], yo[u][:, hlf * 128:(hlf + 1) * 128], cx.ident[:], start=True, stop=True),
                         reads=[byo[u], cx.bconst], pw=[bpYT] if hlf else (), writes=[bpYT] if hlf == 0 else ())
                p.op("act", lambda e: e.activation(out=ybuf[yk][:, :, cs], in_=pYT[:], func=AF.Copy), reads=[bpYT], pw=[bybuf[yk]])
                p.op("pe", lambda e: e.matmul(pSt[:], Btm[u][:], Xd[u][:], start=True, stop=True), reads=[bBtm[u], bXd[u]], writes=[bpSt])
                p.op("dve", lambda e: e.tensor_tensor(out=st[:].rearrange("p (h q) -> p h q", h=4), in0=st[:].rearrange("p (h q) -> p h q", h=4),
                                                      in1=eend[:, ch, :].unsqueeze(2).to_broadcast([128, 4, 64]), op=ALU.mult),
                     reads=[bst, beend], writes=[bst])
                p.op("dve", lambda e: e.tensor_tensor(out=st[:], in0=st[:], in1=pSt[:], op=ALU.add), reads=[bst, bpSt], writes=[bst])
                p.op("act", lambda e: e.activation(out=stb[:], in_=st[:], func=AF.Copy), reads=[bst], writes=[bstb])
            for hlf in range(2):
                p.dma("sp", cx.ex2f[hlf * 128:(hlf + 1) * 128, sc * SC:(sc + 1) * SC], ybuf[yk][:, hlf, :], reads=[bybuf[yk]], pw=[cx.bex2f], track=bybuf[yk])
    if not _os.environ.get('B2_NOAG'):
        p.allgather(cx.g2f, cx.ex2f, reads=[cx.bex2f], writes=[cx.bg2f], track=cx.bg2f)


def phase_C(p, cx, l, xsrc, bxsrc):
    bpar = cx.bpar[l]
    TH = 1024; NH = TC // TH; NTH = TH // TT
    W = cx.W[l]; bW = cx.bW[l]
    for hf in range(NH):
        t0 = hf * TH
        with Scope(p) as s:
            actA = s.sb("actA", [128, 8, TH], BF16); bactA = [Buf() for _ in range(8)]
            actB = s.sb("actB", [128, 16, TH], BF16); bactB = [Buf() for _ in range(16)]
            actC = s.sb("actC", [128, 8, TH], BF16); bactC = Buf()
            merged = s.sb("merged", [128, 16, TH], BF16); bmer = [Buf() for _ in range(16)]
            with Scope(p) as s0:
                cand = [s0.sb("ccand%d" % i, [128, 8, TT], BF16) for i in range(2)]; bcand = [Buf() for _ in range(2)]
                psel = [s0.ps("cpsel%d" % i, [128, TT]) for i in range(2)]; bpsel = [Buf() for _ in range(2)]
                n = 0
                for c in range(8):
                    for t in range(NTH):
                        k = n % 2; n += 1
                        src = cx.g2b[c * 128:(c + 1) * 128, :].rearrange("q (r t) -> q r t", r=8)[:, :, t0 + t * TT:t0 + (t + 1) * TT]
                        p.dma("sp", cand[k][:], src, reads=[cx.bg2b], writes=[bcand[k]])
                        for r in range(8):
                            p.op("pe", lambda e: e.matmul(psel[k][:], cx.selB[:, r, :], cand[k][:, r, :], start=(r == 0), stop=(r == 7)),
                                 reads=[cx.bconst, bcand[k]], writes=[bpsel[k]], accum=True, inc=(r == 7))
                        p.op("act", lambda e: e.activation(out=actC[:, c, t * TT:(t + 1) * TT], in_=psel[k][:], func=AF.Copy), reads=[bpsel[k]], pw=[bactC])
            with Scope(p) as s1:
                hc = s1.sb("hc", [128, 8, TH]); bhc = [Buf() for _ in range(8)]
                hp = [s1.sb("hp%d" % i, [128, 30 + TH]) for i in range(2)]; bhp = [Buf() for _ in range(2)]
                sq = s1.sb("csq", [128, TH]); bsq = Buf()
                hh = s1.sb("hh", [128, 8, HALO_H]); bhh = Buf()
                if hf == 0:
                    gha = s1.sb("gha", [128, 8, 8, HALO_H]); bgha = Buf()
                    p.dma("sp", gha[:].rearrange("q r c w -> q r (c w)"), cx.gh.rearrange("(r q c) w -> q r (c w)", r=8, c=8), reads=[cx.bgh], writes=[bgha])
                    for r in range(8):
                        if r == 0:
                            p.op("dve", lambda e: e.tensor_scalar(out=hh[:], in0=gha[:, r, :, :], scalar1=cx.flg[:, FP + r:FP + r + 1], scalar2=None, op0=ALU.mult),
                                 reads=[bgha, cx.bconst], writes=[bhh])
                        else:
                            p.op("dve", lambda e: e.scalar_tensor_tensor(out=hh[:], in0=gha[:, r, :, :], scalar=cx.flg[:, FP + r:FP + r + 1], in1=hh[:], op0=ALU.mult, op1=ALU.add),
                                 reads=[bgha, cx.bconst, bhh], writes=[bhh])
                mean = s1.sb("mean", [128, TH]); bmean = Buf()
                rstd = s1.sb("crstd", [128, TH]); brstd = Buf()
                tmpc = s1.sb("tmpc", [128, TH]); btmpc = Buf()
                pS1 = [s1.ps("pS1_%d" % i, [128, TT]) for i in range(NTH)]; bpS1 = [Buf() for _ in range(NTH)]
                pS2 = [s1.ps("pS2_%d" % i, [128, TT]) for i in range(NTH)]; bpS2 = [Buf() for _ in range(NTH)]
                cdw = cx.P(l, "cdw"); cdb = cx.P(l, "cdb"); lnw = cx.P(l, "lnw"); lnb = cx.P(l, "lnb")
                for c in range(8):
                    k = c % 2
                    rows = slice(c * 128, (c + 1) * 128)
                    if hf == 0:
                        p.op("dve", lambda e: e.tensor_copy(out=hp[k][:, 0:30], in_=hh[:, c, HALO_H - 30:HALO_H]), reads=[bhh], writes=[bhp[k]])
                        p.dma("sp", hp[k][:, 30:30 + TH], cx.hT[rows, 0:TH], reads=[cx.bhT], pw=[bhp[k]], track=bhp[k])
                    else:
                        p.dma("sp", hp[k][:], cx.hT[rows, t0 - 30:t0 + TH], reads=[cx.bhT], writes=[bhp[k]])
                    p.op("dve", lambda e: e.tensor_scalar(out=hc[:, c, :], in0=hp[k][:, 0:TH], scalar1=cdw[:, c * 31:c * 31 + 1], scalar2=cdb[:, c:c + 1],
                                                          op0=ALU.mult, op1=ALU.add), reads=[bhp[k], bpar], writes=[bhc[c]])
                    for tap in range(1, 31):
                        p.op("dve", lambda e: e.scalar_tensor_tensor(out=hc[:, c, :], in0=hp[k][:, tap:tap + TH], scalar=cdw[:, c * 31 + tap:c * 31 + tap + 1],
                                                                     in1=hc[:, c, :], op0=ALU.mult, op1=ALU.add),
                             reads=[bhp[k], bpar, bhc[c]], writes=[bhc[c]])
                    p.op("act", lambda e: e.activation(out=sq[:], in_=hc[:, c, :], func=AF.Square), reads=[bhc[c]], writes=[bsq])
                    for t in range(NTH):
                        p.op("pe", lambda e: e.matmul(pS1[t][:], cx.ones[:], hc[:, c, t * TT:(t + 1) * TT], start=(c == 0), stop=(c == 7)),
                             reads=[cx.bconst, bhc[c]], writes=[bpS1[t]], accum=True)
                        p.op("pe", lambda e: e.matmul(pS2[t][:], cx.ones[:], sq[:, t * TT:(t + 1) * TT], start=(c == 0), stop=(c == 7)),
                             reads=[cx.bconst, bsq], writes=[bpS2[t]], accum=True)
                for t in range(NTH):
                    ts_ = slice(t * TT, (t + 1) * TT)
                    p.op("act", lambda e: e.activation(out=mean[:, ts_], in_=pS1[t][:], func=AF.Copy, scale=1.0 / 1024), reads=[bpS1[t]], pw=[bmean])
                    p.op("dve", lambda e: e.tensor_tensor(out=tmpc[:, ts_], in0=mean[:, ts_], in1=mean[:, ts_], op=ALU.mult), reads=[bmean], pw=[btmpc])
                    p.op("dve", lambda e: e.scalar_tensor_tensor(out=rstd[:, ts_], in0=pS2[t][:], scalar=1.0 / 1024, in1=tmpc[:, ts_], op0=ALU.mult, op1=ALU.subtract),
                         reads=[bpS2[t], btmpc], pw=[brstd])
                p.op("act", lambda e: e.activation(out=rstd[:], in_=rstd[:], func=AF.Ln, bias=cx.eps5[:, 0:1]), reads=[brstd, cx.bconst], writes=[brstd])
                p.op("act", lambda e: e.activation(out=rstd[:], in_=rstd[:], func=AF.Exp, scale=-0.5), reads=[brstd], writes=[brstd])
                for c in range(8):
                    p.op("dve", lambda e: e.tensor_tensor(out=hc[:, c, :], in0=hc[:, c, :], in1=mean[:], op=ALU.subtract), reads=[bhc[c], bmean], writes=[bhc[c]])
                    p.op("pool", lambda e: e.tensor_tensor(out=hc[:, c, :], in0=hc[:, c, :], in1=rstd[:], op=ALU.mult), reads=[bhc[c], brstd], writes=[bhc[c]])
                    p.op("act", lambda e: e.activation(out=actA[:, c, :], in_=hc[:, c, :], func=AF.Silu, scale=lnw[:, c:c + 1], bias=lnb[:, c:c + 1]),
                         reads=[bhc[c], bpar], writes=[bactA[c]])
            with Scope(p) as s2:
                yz = s2.sb("yz", [128, 4, TH]); byz = [Buf() for _ in range(4)]
                zt = [s2.sb("zt%d" % i, [128, TH]) for i in range(2)]; bzt = [Buf() for _ in range(2)]
                sq2 = s2.sb("sq2", [128, TH]); bsq2 = Buf()
                cnd = [s2.sb("ccnd%d" % i, [128, TH]) for i in range(3)]; bcnd = [Buf() for _ in range(3)]; ncnd = 0
                rs2 = s2.sb("rs2", [128, TH]); brs2 = Buf()
                pG = [s2.ps("pG%d" % i, [128, TT]) for i in range(NTH)]; bpG = [Buf() for _ in range(NTH)]
                snw = cx.P(l, "snw")
                for g in range(4):
                    for cc in range(4):
                        c = g * 4 + cc; k = c % 2
                        for r in range(8):
                            k3 = ncnd % 3; ncnd += 1
                            p.dma("sp", cnd[k3][:], cx.g2f[c * 128:(c + 1) * 128, r * TC + t0:r * TC + t0 + TH], reads=[cx.bg2f], writes=[bcnd[k3]])
                            if r == 0:
                                p.op("dve", lambda e: e.tensor_scalar(out=yz[:, cc, :], in0=cnd[k3][:], scalar1=cx.flg[:, FJ + r:FJ + r + 1], scalar2=None, op0=ALU.mult),
                                     reads=[bcnd[k3], cx.bconst], writes=[byz[cc]])
                            else:
                                p.op("dve", lambda e: e.scalar_tensor_tensor(out=yz[:, cc, :], in0=cnd[k3][:], scalar=cx.flg[:, FJ + r:FJ + r + 1], in1=yz[:, cc, :], op0=ALU.mult, op1=ALU.add),
                                     reads=[bcnd[k3], cx.bconst, byz[cc]], writes=[byz[cc]])
                        p.dma("sp", zt[k][:], cx.szT[c * 128:(c + 1) * 128, t0:t0 + TH], reads=[cx.bszT], writes=[bzt[k]])
                        p.op("dve", lambda e: e.tensor_tensor(out=yz[:, cc, :], in0=yz[:, cc, :], in1=zt[k][:], op=ALU.mult), reads=[byz[cc], bzt[k]], writes=[byz[cc]])
                        p.op("act", lambda e: e.activation(out=sq2[:], in_=yz[:, cc, :], func=AF.Square), reads=[byz[cc]], writes=[bsq2])
                        for t in range(NTH):
                            p.op("pe", lambda e: e.matmul(pG[t][:], cx.ones[:], sq2[:, t * TT:(t + 1) * TT], start=(cc == 0), stop=(cc == 3)),
                                 reads=[cx.bconst, bsq2], writes=[bpG[t]], accum=True)
                    for t in range(NTH):
                        p.op("act", lambda e: e.activation(out=rs2[:, t * TT:(t + 1) * TT], in_=pG[t][:], func=AF.Ln, scale=1.0 / 512, bias=cx.eps6[:, 0:1]),
                             reads=[bpG[t], cx.bconst], pw=[brs2] if t else (), writes=[brs2] if t == 0 else ())
                    p.op("act", lambda e: e.activation(out=rs2[:], in_=rs2[:], func=AF.Exp, scale=-0.5), reads=[brs2], writes=[brs2])
                    for cc in range(4):
                        c = g * 4 + cc
                        p.op("dve", lambda e: e.scalar_tensor_tensor(out=actB[:, c, :], in0=yz[:, cc, :], scalar=snw[:, c:c + 1], in1=rs2[:], op0=ALU.mult, op1=ALU.mult),
                             reads=[byz[cc], bpar, brs2], writes=[bactB[c]])
            with Scope(p) as s3:
                wsA = WStream(p, s3, "wcA", 8 * 128, 2); wsB = WStream(p, s3, "wcB", 16 * 128, 2); wsC = WStream(p, s3, "wcC", 8 * 128, 2)
                pbr = [[s3.ps("pbr%d_%d" % (b, t), [128, TT]) for t in range(NTH)] for b in range(3)]
                bpbr = [[Buf() for t in range(NTH)] for b in range(3)]
                gt = [s3.sb("gt%d" % b, [128, TH]) for b in range(3)]; bgt = [Buf() for _ in range(3)]
                macc = s3.sb("macc", [128, TH]); bmacc = Buf()
                mt = s3.sb("mt", [128, TH]); bmt = Buf()
                wsA.prefetch(W["wco"][0], bW["wco"]); wsB.prefetch(W["wso"][0], bW["wso"]); wsC.prefetch(W["wdo"][0], bW["wdo"])
                for oc in range(16):
                    if oc + 1 < 16:
                        wsA.prefetch(W["wco"][oc + 1], bW["wco"]); wsB.prefetch(W["wso"][oc + 1], bW["wso"]); wsC.prefetch(W["wdo"][oc + 1], bW["wdo"])
                    for b in range(3):
                        p.dma("sp", gt[b][:], cx.gT[b * 2048 + oc * 128:b * 2048 + (oc + 1) * 128, t0:t0 + TH], reads=[cx.bgT], writes=[bgt[b]])
                    for b, (ws, nk, act, bact) in enumerate(((wsA, 8, actA, bactA), (wsB, 16, actB, bactB), (wsC, 8, actC, None))):
                        wt, bw = ws.pop()
                        wv = wt[:].rearrange("p (k j) -> p k j", j=128)
                        for t in range(NTH):
                            for kc in range(nk):
                                p.op("pe", lambda e: e.matmul(pbr[b][t][:], wv[:, kc, :], act[:, kc, t * TT:(t + 1) * TT], start=(kc == 0), stop=(kc == nk - 1)),
                                     reads=[bw, bact[kc] if bact is not None else bactC], writes=[bpbr[b][t]], accum=True, inc=(kc == nk - 1))
                    for t in range(NTH):
                        ts_ = slice(t * TT, (t + 1) * TT)
                        p.op("dve", lambda e: e.tensor_tensor(out=macc[:, ts_], in0=pbr[0][t][:], in1=gt[0][:, ts_], op=ALU.mult), reads=[bpbr[0][t], bgt[0]], pw=[bmacc] if t else (), writes=[bmacc] if t == 0 else ())
                        p.op("dve", lambda e: e.tensor_tensor(out=mt[:, ts_], in0=pbr[1][t][:], in1=gt[1][:, ts_], op=ALU.mult), reads=[bpbr[1][t], bgt[1]], pw=[bmt] if t else (), writes=[bmt] if t == 0 else ())
                    p.op("pool", lambda e: e.tensor_tensor(out=macc[:], in0=macc[:], in1=mt[:], op=ALU.add), reads=[bmacc, bmt], writes=[bmacc])
                    for t in range(NTH):
                        ts_ = slice(t * TT, (t + 1) * TT)
                        p.op("dve", lambda e: e.tensor_tensor(out=mt[:, ts_], in0=pbr[2][t][:], in1=gt[2][:, ts_], op=ALU.mult), reads=[bpbr[2][t], bgt[2]], pw=[bmt] if t else (), writes=[bmt] if t == 0 else ())
                    p.op("pool", lambda e: e.tensor_tensor(out=merged[:, oc, :], in0=macc[:], in1=mt[:], op=ALU.add), reads=[bmacc, bmt], writes=[bmer[oc]])
            with Scope(p) as s4:
                ws = WStream(p, s4, "wcO", 16 * 128, 3)
                po = [s4.ps("po%d" % i, [128, TT]) for i in range(4)]; bpo = [Buf() for _ in range(4)]
                pq = [s4.ps("pq%d" % i, [128, TT]) for i in range(NTH)]; bpq = [Buf() for _ in range(NTH)]
                xt = [s4.sb("cxt%d" % i, [128, TH]) for i in range(2)]; bxt = [Buf() for _ in range(2)]
                xm = [s4.sb("cxm%d" % i, [128, TH]) for i in range(2)]; bxm = [Buf() for _ in range(2)]
                xsq = s4.sb("cxsq", [128, TH]); bxsq = Buf()
                ws.prefetch(W["wo"][0], bW["wo"]); ws.prefetch(W["wo"][1], bW["wo"])
                pc = 0
                for oc in range(16):
                    if oc + 2 < 16:
                        ws.prefetch(W["wo"][oc + 2], bW["wo"])
                    wt, bw = ws.pop()
                    wv = wt[:].rearrange("p (k j) -> p k j", j=128)
                    k = oc % 2
                    p.dma("sp", xt[k][:], xsrc[oc * 128:(oc + 1) * 128, t0:t0 + TH], reads=[bxsrc], writes=[bxt[k]])
                    for t in range(NTH):
                        a = pc % 4; pc += 1
                        ts_ = slice(t * TT, (t + 1) * TT)
                        for kc in range(16):
                            p.op("pe", lambda e: e.matmul(po[a][:], wv[:, kc, :], merged[:, kc, ts_], start=(kc == 0), stop=(kc == 15)),
                                 reads=[bw, bmer[kc]], writes=[bpo[a]], accum=True, inc=(kc == 15))
                        p.op("dve", lambda e: e.tensor_tensor(out=xm[k][:, ts_], in0=po[a][:], in1=xt[k][:, ts_], op=ALU.add),
                             reads=[bpo[a], bxt[k]], pw=[bxm[k]] if t else (), writes=[bxm[k]] if t == 0 else ())
                    p.dma("sp", cx.xmid[oc * 128:(oc + 1) * 128, t0:t0 + TH], xm[k][:], reads=[bxm[k]], pw=[cx.bxmid], track=bxm[k])
                    if hf == NH - 1:
                        p.dma("sp", cx.exx.rearrange("(q c) w -> q c w", c=16)[:, oc, :], xm[k][:, TH - HALO_X:TH], reads=[bxm[k]], pw=[cx.bexx], track=bxm[k])
                    p.op("act", lambda e: e.activation(out=xsq[:], in_=xm[k][:], func=AF.Square), reads=[bxm[k]], writes=[bxsq])
                    for t in range(NTH):
                        p.op("pe", lambda e: e.matmul(pq[t][:], cx.ones[:], xsq[:, t * TT:(t + 1) * TT], start=(oc == 0), stop=(oc == 15)),
                             reads=[cx.bconst, bxsq], writes=[bpq[t]], accum=True)
                for t in range(NTH):
                    p.op("act", lambda e: e.activation(out=cx.rstd2[:, t0 + t * TT:t0 + (t + 1) * TT], in_=pq[t][:], func=AF.Ln, scale=1.0 / D, bias=cx.eps6[:, 0:1]),
                         reads=[bpq[t], cx.bconst], pw=[cx.brstd2])
                p.op("act", lambda e: e.activation(out=cx.rstd2[:, t0:t0 + TH], in_=cx.rstd2[:, t0:t0 + TH], func=AF.Exp, scale=-0.5),
                     reads=[cx.brstd2], writes=[cx.brstd2])
    p.allgather(cx.gx, cx.exx, reads=[cx.bexx], writes=[cx.bgx], track=cx.bgx)


def phase_D(p, cx, l, xdst, bxdst, last):
    bpar = cx.bpar[l]
    TF = 1024; NF = TC // TF; NTF = TF // TT
    W = cx.W[l]; bW = cx.bW[l]
    n2w = cx.P(l, "n2w"); fdw = cx.P(l, "fdw")
    with Scope(p) as s:
        actT = s.sb("actT", [128, FC, TF], BF16); bact = [Buf() for _ in range(FC)]
        if last:
            rsf = s.sb("rsf", [128, TC]); brsf = Buf()
        for f in range(NF):
            t0 = f * TF
            with Scope(p) as s1:
                xn2 = s1.sb("xn2", [128, 16, HALO_X + TF], BF16); bxn2 = [Buf() for _ in range(16)]
                xl = [s1.sb("dxl%d" % i, [128, HALO_X + TF]) for i in range(2)]; bxl = [Buf() for _ in range(2)]
                rsh = s1.sb("rsh", [128, HALO_X + TF]); brsh = Buf()
                gbuf = [s1.sb("gbuf%d" % i, [128, 2 + TF]) for i in range(2)]; bgbuf = [Buf() for _ in range(2)]
                cacc = s1.sb("cacc", [128, TF]); bcacc = Buf()
                wsU = WStream(p, s1, "wdU", 16 * 128, 4)
                pg = [s1.ps("pg%d" % i, [128, TT]) for i in range(2)]; bpg = [Buf() for _ in range(2)]
                pu = [s1.ps("pu%d" % i, [128, TT]) for i in range(2)]; bpu = [Buf() for _ in range(2)]
                ph = s1.ps("ph", [128, HALO_X]); bph = Buf()
                if f == 0:
                    hx = s1.sb("hx", [128, 16, HALO_X]); bhx = Buf()
                    hsq = s1.sb("hsq", [128, 16, HALO_X]); bhsq = Buf()
                    hrs = s1.sb("hrs", [128, HALO_X]); bhrs = Buf()
                    gxa = s1.sb("gxa", [128, 8, 16, HALO_X]); bgxa = Buf()
                    p.dma("sp", gxa[:].rearrange("q r c w -> q r (c w)"), cx.gx.rearrange("(r q c) w -> q r (c w)", r=8, c=16), reads=[cx.bgx], writes=[bgxa])
                    for r in range(8):
                        if r == 0:
                            p.op("dve", lambda e: e.tensor_scalar(out=hx[:], in0=gxa[:, r, :, :], scalar1=cx.flg[:, FP + r:FP + r + 1], scalar2=None, op0=ALU.mult),
                                 reads=[bgxa, cx.bconst], writes=[bhx])
                        else:
                            p.op("dve", lambda e: e.scalar_tensor_tensor(out=hx[:], in0=gxa[:, r, :, :], scalar=cx.flg[:, FP + r:FP + r + 1], in1=hx[:], op0=ALU.mult, op1=ALU.add),
                                 reads=[bgxa, cx.bconst, bhx], writes=[bhx])
                    p.op("act", lambda e: e.activation(out=hsq[:], in_=hx[:], func=AF.Square), reads=[bhx], writes=[bhsq])
                    for c in range(16):
                        p.op("pe", lambda e: e.matmul(ph[:], cx.ones[:], hsq[:, c, :], start=(c == 0), stop=(c == 15)), reads=[cx.bconst, bhsq], writes=[bph], accum=True)
                    p.op("act", lambda e: e.activation(out=hrs[:], in_=ph[:], func=AF.Ln, scale=1.0 / D, bias=cx.eps6[:, 0:1]), reads=[bph, cx.bconst], writes=[bhrs])
                    p.op("act", lambda e: e.activation(out=hrs[:], in_=hrs[:], func=AF.Exp, scale=-0.5), reads=[bhrs], writes=[bhrs])
                    p.op("dve", lambda e: e.tensor_copy(out=rsh[:, 0:HALO_X], in_=hrs[:]), reads=[bhrs], writes=[brsh])
                    p.op("dve", lambda e: e.tensor_copy(out=rsh[:, HALO_X:], in_=cx.rstd2[:, 0:TF]), reads=[cx.brstd2], pw=[brsh])
                else:
                    p.op("dve", lambda e: e.tensor_copy(out=rsh[:], in_=cx.rstd2[:, t0 - HALO_X:t0 + TF]), reads=[cx.brstd2], writes=[brsh])
                for c in range(16):
                    k = c % 2
                    if f == 0:
                        p.op("pool", lambda e: e.tensor_copy(out=xl[k][:, 0:HALO_X], in_=hx[:, c, :]), reads=[bhx], writes=[bxl[k]])
                        p.dma("sp", xl[k][:, HALO_X:], cx.xmid[c * 128:(c + 1) * 128, 0:TF], reads=[cx.bxmid], pw=[bxl[k]], track=bxl[k])
                    else:
                        p.dma("sp", xl[k][:], cx.xmid[c * 128:(c + 1) * 128, t0 - HALO_X:t0 + TF], reads=[cx.bxmid], writes=[bxl[k]])
                    p.op("dve", lambda e: e.scalar_tensor_tensor(out=xn2[:, c, :], in0=xl[k][:], scalar=n2w[:, c:c + 1], in1=rsh[:], op0=ALU.mult, op1=ALU.mult),
                         reads=[bxl[k], bpar, brsh], writes=[bxn2[c]])
                wup = W["wup"]; bwup = bW["wup"]
                for i in range(2):
                    wsU.prefetch(wup[i], bwup)
                for fc in range(FC):
                    if 2 * fc + 2 < 2 * FC:
                        wsU.prefetch(wup[2 * fc + 2], bwup); wsU.prefetch(wup[2 * fc + 3], bwup)
                    wg, bwg = wsU.pop(); wu, bwu = wsU.pop()
                    wgv = wg[:].rearrange("p (k j) -> p k j", j=128); wuv = wu[:].rearrange("p (k j) -> p k j", j=128)
                    gk = fc % 2
                    for kc in range(16):
                        p.op("pe", lambda e: e.matmul(ph[:, 0:HALO_X], wgv[:, kc, :], xn2[:, kc, 0:HALO_X], start=(kc == 0), stop=(kc == 15)),
                             reads=[bwg, bxn2[kc]], writes=[bph], accum=True, inc=(kc == 15))
                    p.op("act", lambda e: e.activation(out=gbuf[gk][:, 0:2], in_=ph[:, HALO_X - 2:HALO_X], func=AF.Copy), reads=[bph], writes=[bgbuf[gk]])
                    for t in range(NTF):
                        ts_ = slice(HALO_X + t * TT, HALO_X + (t + 1) * TT)
                        for kc in range(16):
                            p.op("pe", lambda e: e.matmul(pg[t][:], wgv[:, kc, :], xn2[:, kc, ts_], start=(kc == 0), stop=(kc == 15)),
                                 reads=[bwg, bxn2[kc]], writes=[bpg[t]], accum=True, inc=(kc == 15))
                        p.op("act", lambda e: e.activation(out=gbuf[gk][:, 2 + t * TT:2 + (t + 1) * TT], in_=pg[t][:], func=AF.Copy), reads=[bpg[t]], pw=[bgbuf[gk]])
                        for kc in range(16):
                            p.op("pe", lambda e: e.matmul(pu[t][:], wuv[:, kc, :], xn2[:, kc, ts_], start=(kc == 0), stop=(kc == 15)),
                                 reads=[bwu, bxn2[kc]], writes=[bpu[t]], accum=True, inc=(kc == 15))
                    p.op("dve", lambda e: e.tensor_scalar(out=cacc[:], in0=gbuf[gk][:, 0:TF], scalar1=fdw[:, fc * 3:fc * 3 + 1], scalar2=None, op0=ALU.mult),
                         reads=[bgbuf[gk], bpar], writes=[bcacc])
                    for tap in (1, 2):
                        p.op("dve", lambda e: e.scalar_tensor_tensor(out=cacc[:], in0=gbuf[gk][:, tap:tap + TF], scalar=fdw[:, fc * 3 + tap:fc * 3 + tap + 1], in1=cacc[:],
                                                                     op0=ALU.mult, op1=ALU.add), reads=[bgbuf[gk], bpar, bcacc], writes=[bcacc])
                    p.op("act", lambda e: e.activation(out=cacc[:], in_=cacc[:], func=AF.Silu), reads=[bcacc], writes=[bcacc])
                    for t in range(NTF):
                        p.op("dve", lambda e: e.tensor_tensor(out=actT[:, fc, t * TT:(t + 1) * TT], in0=pu[t][:], in1=cacc[:, t * TT:(t + 1) * TT], op=ALU.mult),
                             reads=[bpu[t], bcacc], pw=[bact[fc]] if t else (), writes=[bact[fc]] if t == 0 else ())
            with Scope(p) as s2:
                wsD = WStream(p, s2, "wdD", FC * 128, 2)
                pd = [s2.ps("pd%d" % i, [128, TT]) for i in range(4)]; bpd = [Buf() for _ in range(4)]
                pf = [s2.ps("pf%d" % i, [128, TT]) for i in range(NTF)]; bpf = [Buf() for _ in range(NTF)]
                xo = [s2.sb("dxo%d" % i, [128, TF]) for i in range(2)]; bxo = [Buf() for _ in range(2)]
                xr = [s2.sb("dxr%d" % i, [128, TF]) for i in range(2)]; bxr = [Buf() for _ in range(2)]
                fsq = s2.sb("fsq", [128, TF]); bfsq = Buf()
                wdn = W["wdn"]; bwdn = bW["wdn"]
                wsD.prefetch(wdn[0], bwdn)
                pc = 0
                for oc in range(16):
                    if oc + 1 < 16:
                        wsD.prefetch(wdn[oc + 1], bwdn)
                    wd, bwd = wsD.pop()
                    wdv = wd[:].rearrange("p (k j) -> p k j", j=128)
                    k = oc % 2
                    p.dma("sp", xr[k][:], cx.xmid[oc * 128:(oc + 1) * 128, t0:t0 + TF], reads=[cx.bxmid], writes=[bxr[k]])
                    for t in range(NTF):
                        a = pc % 4; pc += 1
                        ts_ = slice(t * TT, (t + 1) * TT)
                        for kc in range(FC):
                            p.op("pe", lambda e: e.matmul(pd[a][:], wdv[:, kc, :], actT[:, kc, ts_], start=(kc == 0), stop=(kc == FC - 1)),
                                 reads=[bwd, bact[kc]], writes=[bpd[a]], accum=True, inc=(kc == FC - 1))
                        p.op("dve", lambda e: e.tensor_tensor(out=xo[k][:, ts_], in0=pd[a][:], in1=xr[k][:, ts_], op=ALU.add),
                             reads=[bpd[a], bxr[k]], pw=[bxo[k]] if t else (), writes=[bxo[k]] if t == 0 else ())
                    p.dma("sp", xdst[oc * 128:(oc + 1) * 128, t0:t0 + TF], xo[k][:], reads=[bxo[k]], pw=[bxdst], track=bxo[k])
                    if last:
                        p.op("act", lambda e: e.activation(out=fsq[:], in_=xo[k][:], func=AF.Square), reads=[bxo[k]], writes=[bfsq])
                        for t in range(NTF):
                            p.op("pe", lambda e: e.matmul(pf[t][:], cx.ones[:], fsq[:, t * TT:(t + 1) * TT], start=(oc == 0), stop=(oc == 15)),
                                 reads=[cx.bconst, bfsq], writes=[bpf[t]], accum=True)
                if last:
                    for t in range(NTF):
                        p.op("act", lambda e: e.activation(out=rsf[:, t0 + t * TT:t0 + (t + 1) * TT], in_=pf[t][:], func=AF.Ln, scale=1.0 / D, bias=cx.eps6[:, 0:1]),
                             reads=[bpf[t], cx.bconst], pw=[brsf])
                    p.op("act", lambda e: e.activation(out=rsf[:, t0:t0 + TF], in_=rsf[:, t0:t0 + TF], func=AF.Exp, scale=-0.5), reads=[brsf], writes=[brsf])
        if last:
            with Scope(p) as s3:
                xo = [s3.sb("fxo%d" % i, [128, TF]) for i in range(2)]; bxo = [Buf() for _ in range(2)]
                xr = [s3.sb("fxr%d" % i, [128, TF]) for i in range(2)]; bxr = [Buf() for _ in range(2)]
                fnw = cx.P(l, "fnw")
                for c in range(16):
                    for f in range(NF):
                        k = (c * NF + f) % 2
                        t0 = f * TF
                        p.dma("sp", xr[k][:], xdst[c * 128:(c + 1) * 128, t0:t0 + TF], reads=[bxdst], writes=[bxr[k]])
                        p.op("dve", lambda e: e.scalar_tensor_tensor(out=xo[k][:], in0=xr[k][:], scalar=fnw[:, c:c + 1], in1=rsf[:, t0:t0 + TF], op0=ALU.mult, op1=ALU.mult),
                             reads=[bxr[k], bpar, brsf], writes=[bxo[k]])
                        p.dma("sp", cx.yT[c * 128:(c + 1) * 128, t0:t0 + TF], xo[k][:], reads=[bxo[k]], pw=[cx.byT], track=bxo[k])


def build_program(stages=("A", "B1", "B2", "C", "D"), nlayers=DEPTH, debug=()):
    nc = bass.Bass("TRN2", target_bir_lowering=False)

    def din(name, shape, dt=F32):
        return nc.dram_tensor(name, list(shape), dt, kind="ExternalInput").ap()

    def dint(name, shape, dt=F32):
        return nc.dram_tensor(name, list(shape), dt, kind="Internal").ap()

    cx = Ctx()
    xT = din("xT", [D, TC])
    pos = din("pos", [1, TC], I32)
    par_in = din("par", [DEPTH, 128, NPAR])
    const_in = din("consts", [128, 516])
    flg_in = din("flg", [128, NFLG])
    selb_in = din("selb", [128, 8, 128], BF16)
    wsh = {}
    for l in range(DEPTH):
        for nm, n, kw in WSPEC:
            wsh[(l, nm)] = din("%s_%d" % (nm, l), [n // NCORE, 128, kw])
    cx.yT = nc.dram_tensor("yT", [D, TC], F32, kind="ExternalOutput").ap(); cx.byT = Buf("yT")
    cx.W = [dict() for _ in range(DEPTH)]; cx.bW = [dict() for _ in range(DEPTH)]
    for l in range(DEPTH):
        for nm, n, kw in WSPEC:
            cx.W[l][nm] = nc.dram_tensor("g_%s_%d" % (nm, l), [n, 128, kw], F32).ap(); cx.bW[l][nm] = Buf()
    for nm, shape, dt in (("hT", [1024, TC], F32), ("szT", [2048, TC], F32), ("gT", [6144, TC], F32),
                          ("ex1b", [EX1B, TC], BF16), ("g1b", [NCORE * EX1B, TC], BF16),
                          ("ex1f", [EX1F, TC], F32), ("g1f", [NCORE * EX1F, TC], F32),
                          ("exh", [1024, HALO_H], F32), ("gh", [NCORE * 1024, HALO_H], F32),
                          ("ex2b", [128, S], BF16), ("g2b", [NCORE * 128, S], BF16),
                          ("ex2f", [256, S], F32), ("g2f", [NCORE * 256, S], F32),
                          ("xmid", [D, TC], F32), ("exx", [D, HALO_X], F32), ("gx", [NCORE * D, HALO_X], F32),
                          ("xcur", [D, TC], F32)):
        setattr(cx, nm, dint(nm, shape, dt)); setattr(cx, "b" + nm, Buf(nm))

    with ExitStack() as st:
        p = Prog(nc, st)

        def gsb(name, shape, dt=F32):
            return st.enter_context(nc.sbuf_tensor(name, list(shape), dt))
        for l in range(nlayers):
            for nm, n, kw in WSPEC:
                bnc = nc.dram_tensor("b_%s_%d" % (nm, l), [n // NCORE, 128, kw], F32).ap(); bb = Buf()
                p.dma("sp", bnc, wsh[(l, nm)], writes=[bb])
                p.allgather(cx.W[l][nm], bnc, reads=[bb], writes=[cx.bW[l][nm]], track=cx.bW[l][nm])
        cx.par = gsb("par_sb", [128, DEPTH, NPAR]); cx.bpar = [Buf("par%d" % l) for l in range(DEPTH)]
        for l in range(DEPTH):
            p.dma("sp", cx.par[:, l, :], par_in[l], writes=[cx.bpar[l]])
        cx.P = lambda l, nm: cx.par[:, l, POFF[nm][0]:POFF[nm][0] + POFF[nm][1]]
        cst = gsb("cst", [128, 516]); cx.bconst = Buf("const")
        p.dma("sp", cst[:], const_in, writes=[cx.bconst])
        cx.tri = cst[:, 0:128]; cx.trimask = cst[:, 128:256]; cx.ident = cst[:, 256:384]; cx.sel = cst[:, 384:512]; cx.selT = cst[:, 512:516]
        cx.ones = gsb("ones", [128, 128]); cx.identb = gsb("identb", [128, 128], BF16)
        cx.eps6 = gsb("eps6", [128, 1]); cx.eps5 = gsb("eps5", [128, 1]); cx.one1 = gsb("one1", [128, 1])
        p.op("dve", lambda e: e.memset(cx.ones[:], 1.0), pw=[cx.bconst])
        p.op("dve", lambda e: e.memset(cx.eps6[:], 1e-6), pw=[cx.bconst])
        p.op("dve", lambda e: e.memset(cx.eps5[:], 1e-5), pw=[cx.bconst])
        p.op("dve", lambda e: e.memset(cx.one1[:], 1.0), pw=[cx.bconst])
        p.op("dve", lambda e: e.tensor_copy(out=cx.identb[:], in_=cx.ident), reads=[cx.bconst], pw=[cx.bconst])
        cx.flg = gsb("flg_sb", [128, NFLG]); cx.selB = gsb("selb_sb", [128, 8, 128], BF16)
        p.dma("sp", cx.flg[:], flg_in, pw=[cx.bconst], track=cx.bconst)
        p.dma("sp", cx.selB[:], selb_in, pw=[cx.bconst], track=cx.bconst)
        cx.rstd2 = gsb("rstd2", [128, TC]); cx.brstd2 = Buf()
        p.barrier()
        cx.pos = pos
        bxT = Buf("xT")
        for l in range(nlayers):
            xsrc, bxsrc = (xT, bxT) if l == 0 else (cx.xcur, cx.bxcur)
            last = (l == nlayers - 1)
            if "A" in stages: phase_A(p, cx, l, xsrc, bxsrc)
            if "B1" in stages: phase_B1(p, cx, l)
            if "B2" in stages: phase_B2(p, cx, l)
            if "C" in stages: phase_C(p, cx, l, xsrc, bxsrc)
            if "D" in stages: phase_D(p, cx, l, cx.xcur, cx.bxcur, last)
        for nm in debug:
            src = getattr(cx, nm); bsrc = getattr(cx, "b" + nm)
            dbg = nc.dram_tensor("dbg_" + nm, list(src.shape), src.dtype, kind="ExternalOutput").ap(); bd = Buf()
            p.dma("sp", dbg, src, reads=[bsrc], writes=[bd])
        p.barrier()
    return nc


def prep_inputs(inp):
    x = np.asarray(inp["x"], np.float32)[0]
    pos = np.asarray(inp["positions"], np.int32)
    wfull = {}
    for l in range(DEPTH):
        Win = np.asarray(inp["w_in"][l], np.float32)
        win = np.zeros((NCH_A_PAD, 128, KC * 128), np.float32)
        for ci, (_, _, cols) in enumerate(A_CHUNKS):
            win[ci] = lay_cols(Win, cols)
        wfull[(l, "win")] = win
        wfull[(l, "wco")] = lay_linear(np.asarray(inp["conv_out_w"][l], np.float32))
        wfull[(l, "wso")] = lay_linear(np.asarray(inp["ssd_out_w"][l], np.float32))
        wfull[(l, "wdo")] = lay_linear(np.asarray(inp["da_out_w"][l], np.float32))
        wfull[(l, "wo")] = lay_linear(np.asarray(inp["w_o"][l], np.float32))
        order = np.stack([np.arange(FC), FC + np.arange(FC)], axis=1).reshape(-1)
        wfull[(l, "wup")] = lay_linear(np.asarray(inp["ffn_up_w"][l], np.float32), order)
        wfull[(l, "wdn")] = lay_linear(np.asarray(inp["ffn_down_w"][l], np.float32))
    in_maps = []
    for c in range(NCORE):
        sl = slice(c * TC, (c + 1) * TC)
        m = {"xT": np.ascontiguousarray(x[sl].T), "pos": np.ascontiguousarray(pos[:, sl]),
             "par": np.stack([build_par(inp, l, c) for l in range(DEPTH)]),
             "consts": build_consts(c), "flg": build_flags(c), "selb": build_selb(c)}
        for l in range(DEPTH):
            for nm, n, kw in WSPEC:
                k = n // NCORE
                m["%s_%d" % (nm, l)] = np.ascontiguousarray(wfull[(l, nm)][c * k:(c + 1) * k])
        in_maps.append(m)
    return in_maps


def kernel(**inputs):
    inp = {k: np.asarray(v) for k, v in inputs.items()}
    in_maps = prep_inputs(inp)
    nc = build_program()
    res = run_bass_kernel_spmd(nc, in_maps, core_ids=list(range(NCORE)))
    out = np.concatenate([np.asarray(res.results[c]["yT"], np.float32).T for c in range(NCORE)], axis=0)
    return out.reshape(1, S, D).astype(np.float32)
```
